# Optimizing a Trainium2 kernel written in Bass

```python
import jax
import jax.numpy as jnp
from jax import lax
import numpy as np

D_MODEL = 4096
BATCH = 1
SEQ = 8192
DEPTH = 4

GRID_W = 64
CTX_LEN = 256
HEAD_DIM = 128
N_Q_HEADS = 8
N_KV_HEADS = 2
GQA_GROUP = N_Q_HEADS // N_KV_HEADS
Q_W = N_Q_HEADS * HEAD_DIM
KV_W = N_KV_HEADS * HEAD_DIM
WINDOW = 128
ATTN_BLOCK = 128
ROPE_THETA = 10000.0
SGU_GROUPS = 8
SGU_CH = 128
SGU_W = SGU_GROUPS * SGU_CH
CHUNK = 128
CONV_W = 1024
CONV_K = 3
FOURIER_GROUPS = 4
FOURIER_CH = 256
FOURIER_W = FOURIER_GROUPS * FOURIER_CH
N_BRANCH = 4
BRANCH_W = 1024
D_FF = 3584
ADA_RANK = 256
N_MOD = 9
NORM_EPS = 1e-6
MASK_VALUE = -1e30
_COL_SIZES = (Q_W, KV_W, KV_W, SGU_W, SGU_W, CONV_W, CONV_W, CONV_W, FOURIER_W, N_BRANCH * D_MODEL)
IN_COLS = sum(_COL_SIZES)

kernel_name = 'hybrid_flow_prefix_trunk'


def rms_norm(x, g):
    x32 = x.astype(jnp.float32)
    y = x32 * lax.rsqrt(jnp.mean(x32 * x32, axis=-1, keepdims=True) + NORM_EPS)
    return (y * g.astype(jnp.float32)).astype(x.dtype)


def modulate(xn, shift, scale):
    return xn * (1.0 + scale[:, None, :]) + shift[:, None, :]


def ada_mod(cvec, down, up, bias):
    h = (jax.nn.silu(cvec) @ down) @ up + bias
    return h.reshape(cvec.shape[0], N_MOD, D_MODEL)


def mod_in(h, g, m, i):
    return modulate(rms_norm(h, g), m[:, 3 * i], m[:, 3 * i + 1])


def mod_gate(m, i):
    return m[:, 3 * i + 2][:, None, :]


def swiglu(z, wi, wo):
    a, b = jnp.split(z @ wi, 2, axis=-1)
    return (jax.nn.silu(a) * b) @ wo


def split_projection(p):
    offs, acc = [], 0
    for s in _COL_SIZES[:-1]:
        acc += s
        offs.append(acc)
    return jnp.split(p, offs, axis=-1)


def to_heads(t, n_heads):
    return t.reshape(t.shape[:-1] + (n_heads, HEAD_DIM))


def axial_rope_tables(rows):
    n = rows * GRID_W
    row = jnp.repeat(jnp.arange(rows), GRID_W).astype(jnp.float32)
    col = (jnp.arange(n) % GRID_W).astype(jnp.float32)
    ax = HEAD_DIM // 2
    inv_freq = 1.0 / (ROPE_THETA ** (jnp.arange(0, ax, 2, dtype=jnp.float32) / ax))
    ang_r = row[:, None] * inv_freq[None, :]
    ang_c = col[:, None] * inv_freq[None, :]
    emb = jnp.concatenate([ang_r, ang_r, ang_c, ang_c], axis=-1)
    return jnp.cos(emb), jnp.sin(emb)


def apply_rope(t, cos, sin):
    t32 = t.astype(jnp.float32)
    tr = t32.reshape(t.shape[:-1] + (2, 2, HEAD_DIM // 4))
    rot = jnp.stack([-tr[..., 1, :], tr[..., 0, :]], axis=-2).reshape(t.shape)
    return (t32 * cos[:, None, :] + rot * sin[:, None, :]).astype(t.dtype)


def sink_column(sink, shape):
    s = sink.astype(jnp.float32).reshape(N_KV_HEADS, GQA_GROUP)[:, :, None, None]
    return jnp.broadcast_to(s, shape)


def window_attention(q, k, v, kc, vc, sink):
    b, n = q.shape[0], q.shape[1]
    nb = n // ATTN_BLOCK
    n_loc = 3 * ATTN_BLOCK
    n_ctx = kc.shape[1]
    scale = HEAD_DIM ** -0.5
    qb = q.reshape(b, nb, ATTN_BLOCK, N_KV_HEADS, GQA_GROUP, HEAD_DIM)

    def band(t):
        tp = jnp.pad(t, ((0, 0), (ATTN_BLOCK, ATTN_BLOCK), (0, 0), (0, 0)))
        tb = tp.reshape(b, nb + 2, ATTN_BLOCK, N_KV_HEADS, HEAD_DIM)
        return jnp.concatenate([tb[:, :-2], tb[:, 1:-1], tb[:, 2:]], axis=2)

    kw, vw = band(k), band(v)
    s_loc = jnp.einsum('bnqhgd,bnkhd->bnhgqk', qb, kw).astype(jnp.float32) * scale
    blk = jnp.arange(nb)[:, None]
    qpos = blk * ATTN_BLOCK + jnp.arange(ATTN_BLOCK)[None, :]
    kpos = (blk - 1) * ATTN_BLOCK + jnp.arange(n_loc)[None, :]
    dist = kpos[:, None, :] - qpos[:, :, None]
    valid = (jnp.abs(dist) <= WINDOW) & (kpos[:, None, :] >= 0) & (kpos[:, None, :] < n)
    s_loc = jnp.where(valid[None, :, None, None], s_loc, MASK_VALUE)
    s_ctx = jnp.einsum('bnqhgd,blhd->bnhgql', qb, kc).astype(jnp.float32) * scale
    s_sink = sink_column(sink, s_loc.shape[:-1] + (1,))
    p = jax.nn.softmax(jnp.concatenate([s_loc, s_ctx, s_sink], axis=-1), axis=-1).astype(v.dtype)
    o = (jnp.einsum('bnhgqk,bnkhd->bnqhgd', p[..., :n_loc], vw)
         + jnp.einsum('bnhgql,blhd->bnqhgd', p[..., n_loc:n_loc + n_ctx], vc))
    return o.reshape(b, n, Q_W)


def context_attention(qc, kc, vc, sink):
    b, L = qc.shape[0], qc.shape[1]
    qg = qc.reshape(b, L, N_KV_HEADS, GQA_GROUP, HEAD_DIM)
    s = jnp.einsum('blhgd,bmhd->bhglm', qg, kc).astype(jnp.float32) * (HEAD_DIM ** -0.5)
    s_sink = sink_column(sink, s.shape[:-1] + (1,))
    p = jax.nn.softmax(jnp.concatenate([s, s_sink], axis=-1), axis=-1).astype(vc.dtype)
    o = jnp.einsum('bhglm,bmhd->blhgd', p[..., :L], vc)
    return o.reshape(b, L, Q_W)


def spatial_gating(u, v, ln_g, ln_b, w_s, b_s):
    b, t = u.shape[0], u.shape[1]
    vg = jax.nn.gelu(v).reshape(b, t, SGU_GROUPS, SGU_CH).astype(jnp.float32)
    mu = jnp.mean(vg, axis=-1, keepdims=True)
    var = jnp.mean(jnp.square(vg - mu), axis=-1, keepdims=True)
    vn = ((vg - mu) * lax.rsqrt(var + NORM_EPS) * ln_g.reshape(SGU_GROUPS, SGU_CH).astype(jnp.float32)
          + ln_b.reshape(SGU_GROUPS, SGU_CH).astype(jnp.float32))
    vn = vn.astype(v.dtype).reshape(b, t // CHUNK, CHUNK, SGU_GROUPS, SGU_CH)
    mixed = jnp.einsum('gpq,bcqgd->bcpgd', w_s, vn) + b_s.T[:, :, None]
    return jax.nn.gelu(u) * mixed.reshape(b, t, SGU_W)


def short_conv(xi, gate_b, gate_c, w):
    z = gate_c * xi
    y = lax.conv_general_dilated(z, w[:, None, :], window_strides=(1,),
                                 padding=((CONV_K // 2, CONV_K // 2),),
                                 dimension_numbers=('NWC', 'WIO', 'NWC'),
                                 feature_group_count=CONV_W)
    return gate_b * y


def fourier_mix(f):
    b, t = f.shape[0], f.shape[1]
    fg = f.reshape(b, t, FOURIER_GROUPS, FOURIER_CH).astype(jnp.float32)
    y = jnp.fft.fft2(fg, axes=(1, 3), norm='ortho').real
    return y.astype(f.dtype).reshape(b, t, FOURIER_W)


def merge_branches(ys, gate_logits, w_branch, w_out):
    gates = jax.nn.sigmoid(gate_logits.reshape(gate_logits.shape[:-1] + (N_BRANCH, D_MODEL)))
    merged = gates[..., 0, :] * (ys[0] @ w_branch[0])
    for r in range(1, N_BRANCH):
        merged = merged + gates[..., r, :] * (ys[r] @ w_branch[r])
    return merged @ w_out


def mixer(zl, zc, w_in, q_g, k_g, sink, sgu_ln_g, sgu_ln_b, sgu_w, sgu_b, conv_w, w_branch, w_out,
          cos, sin, context_out):
    ql, kl, vl, ul, vgl, cxl, cbl, ccl, fl, gl = split_projection(zl @ w_in)
    qc, kc, vc, uc, vgc, cxc, cbc, ccc, fc, gc = split_projection(zc @ w_in)
    kc_h = rms_norm(to_heads(kc, N_KV_HEADS), k_g)
    vc_h = to_heads(vc, N_KV_HEADS)
    ql_h = apply_rope(rms_norm(to_heads(ql, N_Q_HEADS), q_g), cos, sin)
    kl_h = apply_rope(rms_norm(to_heads(kl, N_KV_HEADS), k_g), cos, sin)
    ys_l = (window_attention(ql_h, kl_h, to_heads(vl, N_KV_HEADS), kc_h, vc_h, sink),
            spatial_gating(ul, vgl, sgu_ln_g, sgu_ln_b, sgu_w, sgu_b),
            short_conv(cxl, cbl, ccl, conv_w),
            fourier_mix(fl))
    out_l = merge_branches(ys_l, gl, w_branch, w_out)
    if not context_out:
        return out_l, None
    qc_h = rms_norm(to_heads(qc, N_Q_HEADS), q_g)
    ys_c = (context_attention(qc_h, kc_h, vc_h, sink),
            spatial_gating(uc, vgc, sgu_ln_g, sgu_ln_b, sgu_w, sgu_b),
            short_conv(cxc, cbc, ccc, conv_w),
            fourier_mix(fc))
    out_c = merge_branches(ys_c, gc, w_branch, w_out)
    return out_l, out_c


def setup_inputs(seed: int = 0) -> dict:
    key = jax.random.key(seed)
    ks = jax.random.split(key, 21)
    f32 = jnp.float32

    def nrm(k, shape, s):
        return jax.random.normal(k, shape, f32) * s

    return {
        'x': nrm(ks[0], (BATCH, SEQ, D_MODEL), 1.0),
        'c': nrm(ks[1], (BATCH, D_MODEL), 1.0),
        'ctx': nrm(ks[2], (BATCH, CTX_LEN, D_MODEL), 1.0),
        'c_ctx': nrm(ks[3], (D_MODEL,), 1.0),
        'ada_down': nrm(ks[4], (DEPTH, D_MODEL, ADA_RANK), D_MODEL ** -0.5),
        'ada_up': nrm(ks[5], (DEPTH, ADA_RANK, N_MOD * D_MODEL), 0.5 * ADA_RANK ** -0.5),
        'ada_b': nrm(ks[6], (DEPTH, N_MOD * D_MODEL), 0.02),
        'norm_g': 1.0 + nrm(ks[7], (DEPTH, 3, D_MODEL), 0.02),
        'ffn_wi': nrm(ks[8], (DEPTH, 2, D_MODEL, 2 * D_FF), D_MODEL ** -0.5),
        'ffn_wo': nrm(ks[9], (DEPTH, 2, D_FF, D_MODEL), D_FF ** -0.5),
        'w_in': nrm(ks[10], (DEPTH, D_MODEL, IN_COLS), D_MODEL ** -0.5),
        'q_norm': 1.0 + nrm(ks[11], (DEPTH, HEAD_DIM), 0.02),
        'k_norm': 1.0 + nrm(ks[12], (DEPTH, HEAD_DIM), 0.02),
        'sink': nrm(ks[13], (DEPTH, N_Q_HEADS), 0.5),
        'sgu_ln_g': 1.0 + nrm(ks[14], (DEPTH, SGU_W), 0.02),
        'sgu_ln_b': nrm(ks[15], (DEPTH, SGU_W), 0.02),
        'sgu_w': nrm(ks[16], (DEPTH, SGU_GROUPS, CHUNK, CHUNK), CHUNK ** -0.5),
        'sgu_b': nrm(ks[17], (DEPTH, SGU_GROUPS, CHUNK), 0.02),
        'conv_w': nrm(ks[18], (DEPTH, CONV_K, CONV_W), CONV_K ** -0.5),
        'w_branch': nrm(ks[19], (DEPTH, N_BRANCH, BRANCH_W, D_MODEL), BRANCH_W ** -0.5),
        'w_out': nrm(ks[20], (DEPTH, D_MODEL, D_MODEL), D_MODEL ** -0.5),
    }


def reference(x, c, ctx, c_ctx, ada_down, ada_up, ada_b, norm_g, ffn_wi, ffn_wo, w_in, q_norm, k_norm,
              sink, sgu_ln_g, sgu_ln_b, sgu_w, sgu_b, conv_w, w_branch, w_out):
    n = x.shape[1]
    rows = n // GRID_W
    cos, sin = axial_rope_tables(rows)
    h, hc = x, ctx
    for l in range(DEPTH):
        last = l == DEPTH - 1
        m = ada_mod(c, ada_down[l], ada_up[l], ada_b[l])
        mc = ada_mod(c_ctx[None, :], ada_down[l], ada_up[l], ada_b[l])
        h = h + 0.5 * mod_gate(m, 0) * swiglu(mod_in(h, norm_g[l, 0], m, 0), ffn_wi[l, 0], ffn_wo[l, 0])
        hc = hc + 0.5 * mod_gate(mc, 0) * swiglu(mod_in(hc, norm_g[l, 0], mc, 0), ffn_wi[l, 0], ffn_wo[l, 0])
        yl, yc = mixer(mod_in(h, norm_g[l, 1], m, 1), mod_in(hc, norm_g[l, 1], mc, 1), w_in[l],
                       q_norm[l], k_norm[l], sink[l], sgu_ln_g[l], sgu_ln_b[l], sgu_w[l], sgu_b[l],
                       conv_w[l], w_branch[l], w_out[l], cos, sin, not last)
        h = h + mod_gate(m, 1) * yl
        if not last:
            hc = hc + mod_gate(mc, 1) * yc
            hc = hc + 0.5 * mod_gate(mc, 2) * swiglu(mod_in(hc, norm_g[l, 2], mc, 2), ffn_wi[l, 1], ffn_wo[l, 1])
        h = h + 0.5 * mod_gate(m, 2) * swiglu(mod_in(h, norm_g[l, 2], m, 2), ffn_wi[l, 1], ffn_wo[l, 1])
    return h
```

```python
import contextlib
import numpy as np
import ml_dtypes
import concourse.bass as bass
import concourse.mybir as mybir
from concourse.bass_utils import run_bass_kernel_spmd

F32 = mybir.dt.float32
BF16 = mybir.dt.bfloat16
AF = mybir.ActivationFunctionType
ALU = mybir.AluOpType
AX = mybir.AxisListType
NPBF = ml_dtypes.bfloat16

NCORES = 8
DEPTH = 4
D = 4096
KD = 32
TL = 1024
TCX = 32
T = TL + TCX
NTS = [(0, 512), (512, 512), (1024, 32)]
NW = 512
DFF = 3584
KF = 28
SEQ = 8192
CTX = 256
HD = 128
EPS = 1e-6
TILE_W = 128
BLK = 32
NR_L = 54
IN_COLS = 24064
GATE0 = 7680
XU = 6
XK = 0
XV = 576
XVC = 1088
XVN = 1344
XZE = 2368
XZ = 4096


def pos_of_rank(r):
    return ((r >> 1) & 1) * 4 + (r >> 2) * 2 + (r & 1)


RANK_AT_POS = [0] * 8
for _r in range(8):
    RANK_AT_POS[pos_of_rank(_r)] = _r


def layer_stream():
    S = []

    def add(name, mat, sub, col0, kc, idx=0, r=0):
        S.append(dict(name=name, mat=mat, sub=sub, col0=col0, kc=kc, nt=(32 if kc > 8 else 8), idx=idx, r=r))

    def ffn(f):
        for s in range(28):
            add("ffn_a", "ffn_wi", f, 128 * s, 32, s)
            add("ffn_b", "ffn_wi", f, DFF + 128 * s, 32, s)
        for j in range(32):
            add("ffn_o", "ffn_wo", f, 128 * j, 28, j)

    ffn(0)
    for i in range(2):
        add("k", "w_in", None, 1024 + 128 * i, 32, i)
    for i in range(2):
        add("v", "w_in", None, 1280 + 128 * i, 32, i)
    for i in range(8):
        add("q", "w_in", None, 128 * i, 32, i)
    for i in range(8):
        add("f", "w_in", None, 6656 + 128 * i, 32, i)
    for i in range(8):
        add("sv", "w_in", None, 2560 + 128 * i, 32, i)
    for i in range(8):
        add("su", "w_in", None, 1536 + 128 * i, 32, i)
    for i in range(8):
        add("cx", "w_in", None, 3584 + 128 * i, 32, i)
        add("cc", "w_in", None, 5632 + 128 * i, 32, i)
        add("cb", "w_in", None, 4608 + 128 * i, 32, i)
    for j in range(32):
        for r in range(4):
            add("gate", "w_in", None, GATE0 + r * D + 128 * j, 32, j, r)
    for j in range(32):
        for r in range(4):
            add("br", "w_branch", r, 128 * j, 8, j, r)
    for j in range(32):
        add("wo", "w_out", None, 128 * j, 32, j)
    ffn(1)
    t = 0
    for i, s in enumerate(S):
        s["t0"] = t
        s["i"] = i
        t += s["nt"]
    assert t <= NR_L * 8 * BLK, t
    return S


STREAM = layer_stream()
LAST_GATE = max(s["i"] for s in STREAM if s["name"] == "gate")
WSL_N = 6
PF_D = WSL_N - 2
R1 = -(-max(s["t0"] + s["nt"] for s in STREAM if s["name"] == "gate") // (8 * BLK))


def pack_weights(inputs, nlayers):
    big = np.zeros((NCORES, nlayers * NR_L, 128, BLK, TILE_W), np.float32)
    for l in range(nlayers):
        for s in STREAM:
            W = inputs[s["mat"]][l]
            if s["sub"] is not None:
                W = W[s["sub"]]
            kc = s["kc"]
            Wc = np.asarray(W[:, s["col0"]:s["col0"] + TILE_W]).reshape(kc, 128, TILE_W)
            t0 = s["t0"]
            i = 0
            while i < kc:
                b = (t0 + i) // BLK
                j = (t0 + i) % BLK
                n = min(kc - i, BLK - j)
                big[b % 8, l * NR_L + b // 8, :, j:j + n, :] = Wc[i:i + n].transpose(1, 0, 2)
                i += n
    return big.reshape(NCORES, nlayers * NR_L, 128, BLK * TILE_W)


def fm(v):
    v = np.asarray(v, np.float32)
    return v.reshape(-1, 128).T.copy()


def host_consts(core):
    c = core
    out = {}
    pos = np.arange(TL) + TL * c
    row = (pos // 64).astype(np.float32)
    col = (pos % 64).astype(np.float32)
    ax = HD // 2
    inv = (1.0 / (np.float32(10000.0) ** (np.arange(0, ax, 2, dtype=np.float32) / np.float32(ax)))).astype(np.float32)
    ang_r = row[:, None] * inv[None, :]
    ang_c = col[:, None] * inv[None, :]
    emb = np.concatenate([ang_r, ang_r, ang_c, ang_c], axis=-1).astype(np.float32)
    cs = np.zeros((128, 2, T), np.float32)
    cs[:, 0, :TL] = np.cos(emb).T
    cs[:, 1, :TL] = np.sin(emb).T
    cs[:, 0, TL:] = 1.0
    out["cs"] = cs
    R = np.zeros((128, 128), np.float32)
    for i in range(128):
        seg = (i // 32) % 2
        if seg == 0:
            R[i + 32, i] = -1.0
        else:
            R[i - 32, i] = 1.0
    out["rot"] = R.astype(NPBF)
    kp = np.arange(128)[:, None]
    qp = np.arange(128)[None, :]
    mP = (kp >= qp).astype(np.float32)
    mN = (kp <= qp).astype(np.float32)
    m = np.zeros((128, 4, 4, 128), np.float32)
    m[:, 0] = mP[:, None, :]
    m[:, 1] = mN[:, None, :]
    m[:, 2] = mP[:, None, :] if c > 0 else 0.0
    m[:, 3] = mN[:, None, :] if c < NCORES - 1 else 0.0
    out["masks"] = m.reshape(128, 4, 512).astype(NPBF)
    oh = np.zeros((128, 2, 8), np.float32)
    if c > 0:
        oh[:, 0, pos_of_rank(c - 1)] = 1.0
    if c < NCORES - 1:
        oh[:, 1, pos_of_rank(c + 1)] = 1.0
    out["oh"] = oh
    ch = np.arange(256)
    ang = 2.0 * np.pi * ((ch[:, None] * ch[None, :]) % 256) / 256.0
    cs256 = np.concatenate([np.cos(ang), np.sin(ang)], axis=1) / 16.0
    out["cs256"] = cs256.reshape(2, 128, 512).transpose(1, 0, 2).astype(NPBF)
    tab = np.arange(SEQ)
    ctab = (np.cos(2.0 * np.pi * tab / SEQ) / np.sqrt(SEQ)).astype(np.float32)
    stab = (-np.sin(2.0 * np.pi * tab / SEQ) / np.sqrt(SEQ)).astype(np.float32)
    s_idx = (np.arange(TL) + TL * c).astype(np.int64)
    t_idx = np.arange(SEQ, dtype=np.int64)
    prod = (t_idx[:, None] * s_idx[None, :]) % SEQ
    cts = np.empty((SEQ, 2, TL), NPBF)
    cts[:, 0, :] = ctab[prod].astype(NPBF)
    cts[:, 1, :] = stab[prod].astype(NPBF)
    out["cts"] = cts.reshape(64, 128, 2, TL).transpose(1, 0, 2, 3).copy()
    tc_ = np.arange(CTX, dtype=np.int64)
    sc_ = (np.arange(TCX) + TCX * c).astype(np.int64)
    pc = (tc_[:, None] * sc_[None, :]) % CTX
    c2 = np.empty((CTX, 2, TCX), np.float32)
    c2[:, 0, :] = np.cos(2.0 * np.pi * pc / CTX) / 16.0
    c2[:, 1, :] = -np.sin(2.0 * np.pi * pc / CTX) / 16.0
    out["ct256"] = c2.reshape(2, 128, 2, TCX).transpose(1, 0, 2, 3).astype(NPBF).copy()
    return out


def host_inputs(inputs, nlayers):
    x = np.asarray(inputs["x"], np.float32)[0]
    ctx = np.asarray(inputs["ctx"], np.float32)[0]
    wts = pack_weights(inputs, nlayers)
    cT = np.stack([fm(inputs["c"][0]), fm(inputs["c_ctx"])], axis=-1)
    adown = np.asarray(inputs["ada_down"], np.float32)[:nlayers].reshape(nlayers, KD, 128, 256).transpose(2, 0, 1, 3).copy()
    ng = np.asarray(inputs["norm_g"], np.float32)[:nlayers].reshape(nlayers, 3, KD, 128).transpose(3, 0, 1, 2).copy()
    qk = np.stack([np.asarray(inputs["q_norm"], np.float32)[:nlayers].T,
                   np.asarray(inputs["k_norm"], np.float32)[:nlayers].T], axis=-1).copy()
    sink = np.asarray(inputs["sink"], np.float32)[:nlayers].reshape(1, nlayers * 8)
    lng = np.asarray(inputs["sgu_ln_g"], np.float32)[:nlayers].reshape(1, nlayers * 1024)
    lnb = np.asarray(inputs["sgu_ln_b"], np.float32)[:nlayers].reshape(1, nlayers * 1024)
    sw = np.asarray(inputs["sgu_w"], np.float32)[:nlayers]
    wsT = sw.transpose(3, 0, 1, 2).copy()
    sb_ = np.asarray(inputs["sgu_b"], np.float32)[:nlayers]
    cw = np.asarray(inputs["conv_w"], np.float32)[:nlayers].reshape(nlayers, 3, 8, 128).transpose(3, 0, 2, 1).copy()
    aup_full = np.asarray(inputs["ada_up"], np.float32)[:nlayers]
    ab_full = np.asarray(inputs["ada_b"], np.float32)[:nlayers]
    maps = []
    for c in range(NCORES):
        tok = np.concatenate([x[TL * c:TL * (c + 1)], ctx[TCX * c:TCX * (c + 1)]], axis=0)
        xT = tok.T.reshape(KD, 128, T).transpose(1, 0, 2).copy()
        aup = aup_full[:, :, 4608 * c:4608 * (c + 1)].reshape(nlayers, 2, 128, 4608).transpose(2, 0, 1, 3).copy()
        ab = ab_full[:, 4608 * c:4608 * (c + 1)].reshape(nlayers, 36, 128).transpose(2, 0, 1).copy()
        wsc = np.zeros((128, nlayers, 2, 8, TCX), np.float32)
        p0 = TCX * (c % 4)
        wsc[:, :, c // 4, :, :] = sw[:, :, p0:p0 + TCX, :].transpose(3, 0, 1, 2)
        bsc = sb_[:, :, p0 + 128 * 0:p0 + TCX].reshape(1, nlayers * 8 * TCX).copy()
        m = dict(xT=xT, wts=wts[c], cT=cT, adown=adown, aup=aup, ab=ab, ng=ng, qk=qk, sink=sink,
                 lng=lng, lnb=lnb, wsT=wsT, wsc=wsc, bsT=sb_.reshape(1, nlayers * 1024).copy(), bsc=bsc, cw=cw)
        m.update(host_consts(c))
        maps.append(m)
    return maps


class Sem:
    def __init__(self, h):
        self.h = h
        self.v = 0

    def inc(self, ins, n=1):
        ins.then_inc(self.h, n)
        self.v += n
        return (self, self.v)


class Ring:
    def __init__(self, b, name, n, shape, dt):
        self.n = n
        self.t = [b.sb(f"{name}{i}", shape, dt) for i in range(n)]
        self.dsem = [b.sem(f"{name}_d{i}") for i in range(n)]
        self.free = [[] for _ in range(n)]
        self.i = 0

    def take(self):
        s = self.i % self.n
        self.i += 1
        return s


class Builder:
    def __init__(self, nlayers=DEPTH, stop="full", dbg=(), nrounds=None):
        self.nlayers = nlayers
        self.stop = stop
        self.dbg = dbg
        self.nrounds = nrounds
        self.nc = bass.Bass("TRN2", target_bir_lowering=False)
        self.es = contextlib.ExitStack()
        self.waited = {}

    def sb(self, name, shape, dt):
        return self.es.enter_context(self.nc.sbuf_tensor(name, list(shape), dt))

    def sem(self, name):
        return Sem(self.es.enter_context(self.nc.semaphore(name)))

    def dram(self, name, shape, dt, kind=None):
        if kind is None:
            return self.nc.dram_tensor(name, list(shape), dt)
        return self.nc.dram_tensor(name, list(shape), dt, kind=kind)

    def wait(self, eng, ev):
        if ev is None:
            return
        if isinstance(ev, list):
            for e in ev:
                self.wait(eng, e)
            return
        sem, v = ev
        key = (id(eng), id(sem))
        if self.waited.get(key, 0) >= v:
            return
        self.waited[key] = v
        eng.wait_ge(sem.h, v)

    def ev(self, eng, ins):
        return self.esem[id(eng)].inc(ins, 1)

    def op(self, eng, waits, ins_fn):
        self.wait(eng, waits)
        return self.ev(eng, ins_fn())

    def dma(self, q, out, in_, sem, waits=None):
        self.wait(q, waits)
        ins = q.dma_start(out=out, in_=in_)
        return sem.inc(ins, 16)

    def mdma(self, q, out, in_, waits=None):
        k = self.misc_i % len(self.misc_sems)
        self.misc_i += 1
        sem = self.misc_sems[k]
        self.wait(q, self.misc_last[k])
        ev = self.dma(q, out, in_, sem, waits)
        self.misc_last[k] = ev
        return ev

    def ps_take(self):
        bnk = self.ps_i % 8
        self.ps_i += 1
        n_try = 0
        while self.ps_held[bnk]:
            bnk = self.ps_i % 8
            self.ps_i += 1
            n_try += 1
            assert n_try < 8, "all psum banks held"
        self.ps_held[bnk] = True
        self.wait(self.pe, self.ps_free[bnk])
        self.ps_free[bnk] = []
        return bnk

    def ps_release(self, bnk, evs):
        self.ps_held[bnk] = False
        self.ps_free[bnk] = list(evs) if isinstance(evs, list) else [evs]

    def pg(self, i, n=1, dt=BF16):
        if n == 1:
            v = self.bufA[:, i, :]
        else:
            v = self.bufA[:, i:i + n, :].rearrange("p a b -> p (a b)")
        return v if dt == BF16 else v.bitcast(dt)

    def build(self):
        with self.es:
            self._build()
        return self.nc

    def _build(self):
        nc = self.nc
        L = self.nlayers
        self.pe, self.act, self.dve, self.pool, self.sp = nc.tensor, nc.scalar, nc.vector, nc.gpsimd, nc.sync
        self.esem = {id(self.pe): self.sem("t_pe"), id(self.act): self.sem("t_act"), id(self.dve): self.sem("t_dve"),
                     id(self.pool): self.sem("t_pool")}
        NRT = L * NR_L
        di = lambda n, s, d=F32: self.dram(n, s, d, "ExternalInput")
        self.xT = di("xT", [128, KD, T])
        self.wts = di("wts", [self.nrounds or NRT, 128, BLK * TILE_W])
        self.cT = di("cT", [128, KD, 2])
        self.adown = di("adown", [128, L, KD, 256])
        self.aup = di("aup", [128, L, 2, 4608])
        self.ab = di("ab", [128, L, 36])
        self.ng = di("ng", [128, L, 3, KD])
        self.qk = di("qk", [128, L, 2])
        self.sink = di("sink", [1, L * 8])
        self.lng = di("lng", [1, L * 1024])
        self.lnb = di("lnb", [1, L * 1024])
        self.wsT = di("wsT", [128, L, 8, 128])
        self.wsc = di("wsc", [128, L, 2, 8, TCX])
        self.bsT = di("bsT", [1, L * 1024])
        self.bsc = di("bsc", [1, L * 8 * TCX])
        self.cw = di("cw", [128, L, 8, 3])
        self.cs = di("cs", [128, 2, T])
        self.rot = di("rot", [128, 128], BF16)
        self.masks = di("masks", [128, 4, 512], BF16)
        self.oh = di("oh", [128, 2, 8])
        self.cs256 = di("cs256", [128, 2, 512], BF16)
        self.cts = di("cts", [128, 64, 2, TL], BF16)
        self.ct256 = di("ct256", [128, 2, 2, TCX], BF16)
        self.outT = self.dram("outT", [128, KD, TL], F32, "ExternalOutput")
        self.wb = self.dram("wb", [NRT, 128, BLK * TILE_W], BF16)
        self.g1 = {}
        self.wg = [self.dram(f"wg{i}", [32, 8, 128, BLK, TILE_W], BF16) for i in range((NRT + 31) // 32)]
        self.hT = self.dram("hT", [128, KD, T], F32)
        self.gT = self.dram("gT", [128, KF, T], BF16)
        self.mT = self.dram("mT", [128, KD, T], BF16)
        self.Pys = self.dram("Pys", [128, KD, T], BF16)
        self.Pq = self.dram("Pq", [128, 8, T], BF16)
        self.Pk = self.dram("Pk", [128, 2, T], BF16)
        self.Pv = self.dram("Pv", [T, 256], BF16)
        self.Pu = self.dram("Pu", [128, 8, T], BF16)
        self.Pvn = self.dram("Pvn", [T, 1024], BF16)
        self.Pcx = self.dram("Pcx", [128, 3, 8, T], F32)
        self.Pgate = self.dram("Pgate", [128, 4, KD, T], BF16)
        self.xin = [self.dram(f"xin{u}", [128, 4096], BF16) for u in range(XU)]
        self.xall = [self.dram(f"xall{u}", [8, 128, 4096], BF16) for u in range(XU)]
        self.xinz = self.dram("xinz", [128, 32], F32)
        self.zall = self.dram("zall", [8, 128, 32], F32)
        self.min_ = self.dram("min_", [128, L * 72], F32)
        self.mall = self.dram("mall", [8, 128, L * 72], F32)
        self.bufA = self.sb("bufA", [128, KD, T], BF16)
        self.wsl = Ring(self, "wsl", WSL_N, [128, 32, TILE_W], BF16)
        self.hch = Ring(self, "hch", 4, [128, T], F32)
        self.hout = Ring(self, "hout", 2, [128, T], F32)
        self.tmpf = Ring(self, "tmpf", 2, [128, T], F32)
        self.sil = Ring(self, "sil", 4, [128, NW], F32)
        self.gst = Ring(self, "gst", 4, [128, T], BF16)
        self.sqb = Ring(self, "sqb", 3, [128, T], BF16)
        self.stf = Ring(self, "stf", 2, [128, 1024], F32)
        self.stb = Ring(self, "stb", 2, [128, 1024], BF16)
        self.rstd = self.sb("rstd", [128, T], F32)
        self.mods = self.sb("mods", [128, L, 288, 2], F32)
        self.ngs = self.sb("ngs", [128, L, 3, KD], F32)
        self.modA = self.sb("modA", [128, 3, KD, 2], F32)
        self.modG = self.sb("modG", [128, 3, KD, 2], F32)
        self.ones_b = self.sb("ones_b", [128, 128], BF16)
        self.eps_t = self.sb("eps_t", [128, 1], F32)
        self.cs256_t = self.sb("cs256_t", [128, 2, 512], BF16)
        self.ct256_t = self.sb("ct256_t", [128, 2, 2, TCX], BF16)
        self.rot_b = self.sb("rot_b", [128, 128], BF16)
        self.qkg = self.sb("qkg", [128, L, 2], F32)
        self.esk = self.sb("esk", [128, L * 8], F32)
        self.cw_t = self.sb("cw_t", [128, L, 8, 3], F32)
        self.oh_t = self.sb("oh_t", [128, 2, 8], F32)
        self.xe = self.sb("xe", [128, 8, 4], F32)
        self.ce = self.sb("ce", [128, 8, 4], F32)
        self.zeo = self.sb("zeo", [128, 32], F32)
        self.selz = self.sb("selz", [128, 2, 32], F32)
        self.stt = self.sb("stt", [128, 16], F32)
        self.ps = self.es.enter_context(nc.psum_tensor("ps", [128, 8, 512], F32))
        self.ps_i = 0
        self.ps_free = [[] for _ in range(8)]
        self.ps_held = [False] * 8
        self.s_cc = self.sem("cc")
        self.s_cc1 = self.sem("cc1")
        self.misc_sems = [self.sem(f"misc{i}") for i in range(12)]
        self.misc_last = [None] * 12
        self.misc_i = 0
        self.round_ev = {}
        self.pf_layer, self.pf_ptr, self.pf_issued, self.pf_limit = -1, 0, {}, LAST_GATE
        self.pool_cc_pending = None
        self.g_n = 0
        self.g_pend = None
        self.g1_busy = [None, None, None]
        self.l1_last = [None, None]
        self.l2_last = [None, None]
        self.s_l1 = [self.sem('cl1a'), self.sem('cl1b')]
        self.s_l2 = [self.sem('cl2a'), self.sem('cl2b')]
        self.bufA_free = []
        self.rstd_free = []
        self.modAG_free = []
        dve, act, sp = self.dve, self.act, self.sp

        e1 = self.ev(dve, dve.memset(self.ones_b[:], 1.0))
        e2 = self.ev(dve, dve.memset(self.eps_t[:], EPS))
        cev = [e1, e2]
        for dst, src in ((self.cs256_t, self.cs256), (self.ct256_t, self.ct256), (self.rot_b, self.rot), (self.qkg, self.qk),
                         (self.cw_t, self.cw), (self.oh_t, self.oh)):
            cev.append(self.mdma(sp, dst[:], src[:]))
        esk_l = self.mdma(sp, self.esk[:], self.sink[0:1, :].partition_broadcast(128))
        cev.append(self.op(act, [esk_l], lambda: act.activation(out=self.esk[:], in_=self.esk[:], func=AF.Exp)))
        self.const_ev = cev
        for e in (self.pe, self.act, self.dve):
            self.wait(e, cev)

        self.h_ev = [self.mdma(sp, self.hT[:], self.xT[:])]
        self.adaln()
        self.pool_rounds(0, min(R1, self.nrounds or R1))
        for l in range(L):
            done = self.layer(l)
            if not done:
                break
        self.finish()

    def pool_rounds(self, t0, t1):
        pool = self.pool
        for t in range(t0, t1):
            oevs = []
            for q in range(4):
                sf = self.stf.take()
                ein = self.dma(pool, self.stf.t[sf][:], self.wts[t, :, 1024 * q:1024 * (q + 1)], self.stf.dsem[sf],
                               waits=self.stf.free[sf])
                sb_ = self.stb.take()
                self.wait(pool, [ein] + self.stb.free[sb_])
                ec = self.ev(pool, pool.tensor_copy(out=self.stb.t[sb_][:], in_=self.stf.t[sf][:]))
                self.stf.free[sf] = [ec]
                eo = self.dma(pool, self.wb[t, :, 1024 * q:1024 * (q + 1)], self.stb.t[sb_][:], self.stb.dsem[sb_], waits=[ec])
                self.stb.free[sb_] = [eo]
                oevs.append(eo)
            self.wait(pool, oevs)
            prev = self.gather(self.wb[t], self.wg[t // 32][t % 32].rearrange("a p b c -> a p (b c)"), BF16)
            if t > t0:
                self.round_ev[t - 1] = prev
        if t1 > t0:
            self.round_ev[t1 - 1] = self.gather_flush()

    def gather(self, src, dst, dt):
        pool = self.pool
        X = src.shape[-1]
        key = (X, str(dt))
        if key not in self.g1:
            self.g1[key] = [self.dram(f"g1_{X}_{len(self.g1)}_{i}", [4 * 128, X], dt) for i in range(3)]
        n = self.g_n
        self.g_n += 1
        g1 = self.g1[key][n % 3]
        self.wait(pool, self.g1_busy[n % 3])
        self.wait(pool, self.l1_last[n % 2])
        i1 = pool.collective_compute("AllGather", ALU.bypass, replica_groups=[[0, 1, 2, 3], [4, 5, 6, 7]],
                                     ins=[src], outs=[g1[:, :]])
        e1 = self.s_l1[n % 2].inc(i1, 1)
        self.l1_last[n % 2] = e1
        prev_done = self._gather_level2()
        self.g_pend = (n, g1, dst, e1)
        return prev_done

    def _gather_level2(self):
        pool = self.pool
        if self.g_pend is None:
            return None
        n, g1, dst, e1 = self.g_pend
        self.g_pend = None
        self.wait(pool, e1)
        self.wait(pool, self.l2_last[n % 2])
        for h in range(2):
            i2 = pool.collective_compute("AllGather", ALU.bypass, replica_groups=[[0, 4], [1, 5], [2, 6], [3, 7]],
                                         ins=[g1[256 * h:256 * (h + 1), :]],
                                         outs=[dst[4 * h:4 * (h + 1)].rearrange("a p x -> (a p) x")])
            e2 = self.s_l2[n % 2].inc(i2, 1)
        self.g1_busy[n % 3] = e2
        self.l2_last[n % 2] = e2
        self.pool_cc_pending = e2
        return e2

    def gather_flush(self):
        return self._gather_level2()

    def adaln(self):
        L = self.nlayers
        act, dve, pe, sp = self.act, self.dve, self.pe, self.sp
        cs = self.sb("ada_c", [128, KD, 2], F32)
        tT = self.sb("ada_t", [128, 2, 2], F32)
        mloc = self.sb("ada_m", [128, L, 36, 2], F32)
        abs_ = self.sb("ada_ab", [128, L, 36], F32)
        e = self.mdma(sp, cs[:], self.cT[:])
        e_ab = self.mdma(sp, abs_[:], self.ab[:])
        e_ng = self.mdma(sp, self.ngs[:], self.ng[:])
        ec = self.op(act, [e], lambda: act.activation(out=cs[:], in_=cs[:], func=AF.Silu))
        for l in range(L):
            for half in range(2):
                bnk = None
                for piece in range(8):
                    s = self.hch.take()
                    wv = self.hch.t[s][:, 0:512].rearrange("p (k m) -> p k m", k=4)
                    el = self.dma(sp, wv, self.adown[:, l, 4 * piece:4 * piece + 4, 128 * half:128 * half + 128],
                                  self.hch.dsem[s], waits=self.hch.free[s])
                    if piece == 0:
                        bnk = self.ps_take()
                    self.wait(pe, [el, ec])
                    for k in range(4):
                        kk = 4 * piece + k
                        mm = pe.matmul(self.ps[:, bnk, 0:2], lhsT=wv[:, k, :], rhs=cs[:, kk, :], start=(kk == 0), stop=(kk == KD - 1))
                    em = self.ev(pe, mm)
                    self.hch.free[s] = [em]
                et = self.op(dve, [em], lambda: dve.tensor_copy(out=tT[:, half, :], in_=self.ps[:, bnk, 0:2]))
                self.ps_release(bnk, et)
            for grp in range(9):
                s = self.hch.take()
                wv = self.hch.t[s][:, 0:1024].rearrange("p (k m) -> p k m", k=2)
                el = self.dma(sp, wv, self.aup[:, l, :, 512 * grp:512 * (grp + 1)], self.hch.dsem[s], waits=self.hch.free[s])
                bnk = self.ps_take()
                self.wait(pe, [el, et])
                for j in range(4):
                    for k in range(2):
                        mm = pe.matmul(self.ps[:, bnk, 2 * j:2 * j + 2], lhsT=wv[:, k, 128 * j:128 * (j + 1)], rhs=tT[:, k, :],
                                       start=(k == 0), stop=(k == 1))
                em = self.ev(pe, mm)
                self.hch.free[s] = [em]
                self.wait(dve, [em, e_ab])
                for j in range(4):
                    ch = 4 * grp + j
                    eo = self.ev(dve, dve.tensor_scalar(out=mloc[:, l, ch, :], in0=self.ps[:, bnk, 2 * j:2 * j + 2],
                                                        scalar1=abs_[:, l, ch:ch + 1], scalar2=None, op0=ALU.add))
                self.ps_release(bnk, eo)
        es_ = self.mdma(sp, self.min_[:, :], mloc[:].rearrange("p l c t -> p (l c t)"), waits=[eo])
        self.wait(self.pool, es_)
        self.gather(self.min_[:, :], self.mall, F32)
        eg = self.gather_flush()
        evs = []
        for pos in range(8):
            r = RANK_AT_POS[pos]
            evs.append(self.mdma(sp, self.mods[:, :, 36 * r:36 * (r + 1), :],
                                 self.mall[pos].rearrange("p (l c t) -> p l c t", l=L, c=36), waits=[eg]))
        self.mods_ev = evs + [e_ng]

    def layer_mods(self, l):
        dve = self.dve
        self.wait(dve, self.mods_ev + self.modAG_free)
        for i in range(3):
            for t in range(2):
                dve.scalar_tensor_tensor(out=self.modA[:, i, :, t], in0=self.mods[:, l, (3 * i + 1) * 32:(3 * i + 2) * 32, t],
                                         scalar=1.0, in1=self.ngs[:, l, i, :], op0=ALU.add, op1=ALU.mult)
                e = self.ev(dve, dve.tensor_scalar(out=self.modG[:, i, :, t], in0=self.mods[:, l, (3 * i + 2) * 32:(3 * i + 3) * 32, t],
                                                   scalar1=(1.0 if i == 1 else 0.5), scalar2=None, op0=ALU.mult))
        self.modAG_ev = e

    def _issue_slab(self, l, s):
        sp = self.sp
        slot = self.wsl.take()
        t0, kc = s["t0"], s["kc"]
        b = t0 // BLK
        j = t0 % BLK
        rnd = l * NR_L + b // 8
        pos = pos_of_rank(b % 8)
        ev = self.dma(sp, self.wsl.t[slot][:, 0:kc, :], self.wg[rnd // 32][rnd % 32, pos, :, j:j + kc, :], self.wsl.dsem[slot],
                      waits=self.wsl.free[slot] + [self.round_ev[rnd]])
        return slot, ev

    def load_slab(self, l, s):
        i = s["i"]
        if self.pf_layer != l:
            self.pf_layer, self.pf_ptr, self.pf_issued = l, 0, {}
        assert i >= self.pf_ptr - len(self.pf_issued), (i, self.pf_ptr)
        target = max(i, min(i + PF_D, self.pf_limit))
        while self.pf_ptr <= target:
            sj = STREAM[self.pf_ptr]
            self.pf_issued[self.pf_ptr] = self._issue_slab(l, sj)
            self.pf_ptr += 1
        return self.pf_issued.pop(i)

    def sl(self, name, sub=None):
        return [s for s in STREAM if s["name"] == name and (sub is None or s["sub"] == sub)]

    def mm_fm(self, slot, lev, kc, c0, w, in_ev, rhs0=0):
        pe = self.pe
        bnk = self.ps_take()
        self.wait(pe, [lev] + in_ev)
        for k in range(kc):
            mm = pe.matmul(self.ps[:, bnk, 0:w], lhsT=self.wsl.t[slot][:, k, :], rhs=self.bufA[:, rhs0 + k, c0:c0 + w],
                           start=(k == 0), stop=(k == kc - 1))
        return bnk, self.ev(pe, mm)

    def mm_tm(self, slot, lev, tb, in_ev):
        pe = self.pe
        M = 128 if tb < 8 else TCX
        bnk = self.ps_take()
        self.wait(pe, [lev] + in_ev)
        for k in range(KD):
            mm = pe.matmul(self.ps[0:M, bnk, 0:128], lhsT=self.bufA[:, k, 128 * tb:128 * tb + M], rhs=self.wsl.t[slot][:, k, :],
                           start=(k == 0), stop=(k == KD - 1))
        return bnk, self.ev(pe, mm), M

    def norm(self, l, i):
        act, dve, pe, sp = self.act, self.dve, self.pe, self.sp
        if getattr(self, "ss_banks", None) is not None:
            banks, em = self.ss_banks, self.ss_ev
            self.ss_banks = None
        else:
            banks = [self.ps_take() for _ in range(3)]
            for kc in range(KD):
                s = self.hch.take()
                el = self.dma(sp, self.hch.t[s][:], self.hT[:, kc, :], self.hch.dsem[s], waits=self.hch.free[s] + self.h_ev)
                q = self.sqb.take()
                ea = self.op(act, [el] + self.sqb.free[q], lambda: act.activation(out=self.sqb.t[q][:], in_=self.hch.t[s][:], func=AF.Square))
                self.hch.free[s] = [ea]
                self.wait(pe, [ea])
                for n, (c0, w) in enumerate(NTS):
                    mm = pe.matmul(self.ps[:, banks[n], 0:w], lhsT=self.ones_b[:], rhs=self.sqb.t[q][:, c0:c0 + w],
                                   start=(kc == 0), stop=(kc == KD - 1))
                em = self.ev(pe, mm)
                self.sqb.free[q] = [em]
        self.wait(act, [em] + self.rstd_free)
        for n, (c0, w) in enumerate(NTS):
            ea = self.ev(act, act.activation(out=self.rstd[:, c0:c0 + w], in_=self.ps[:, banks[n], 0:w], func=AF.Sqrt,
                                             scale=1.0 / D, bias=self.eps_t[:, 0:1]))
        for bnk in banks:
            self.ps_release(bnk, ea)
        er = self.op(dve, [ea], lambda: dve.reciprocal(out=self.rstd[:], in_=self.rstd[:]))
        self.wait(act, [self.modAG_ev] + self.bufA_free)
        for kc in range(KD):
            s = self.hch.take()
            el = self.dma(sp, self.hch.t[s][:], self.hT[:, kc, :], self.hch.dsem[s], waits=self.hch.free[s] + self.h_ev)
            tq = self.tmpf.take()
            ed = self.op(dve, [el, er] + self.tmpf.free[tq],
                         lambda: dve.tensor_tensor(out=self.tmpf.t[tq][:], in0=self.hch.t[s][:], in1=self.rstd[:], op=ALU.mult))
            self.hch.free[s] = [ed]
            self.wait(act, ed)
            sh = 3 * i * 32 + kc
            act.activation(out=self.bufA[:, kc, 0:TL], in_=self.tmpf.t[tq][:, 0:TL], func=AF.Identity,
                           scale=self.modA[:, i, kc:kc + 1, 0], bias=self.mods[:, l, sh:sh + 1, 0])
            ea = self.ev(act, act.activation(out=self.bufA[:, kc, TL:T], in_=self.tmpf.t[tq][:, TL:T], func=AF.Identity,
                                             scale=self.modA[:, i, kc:kc + 1, 1], bias=self.mods[:, l, sh:sh + 1, 1]))
            self.tmpf.free[tq] = [ea]
        self.rstd_free = [ed]
        self.bufA_ev = [ea]
        self.modAG_free = [ea]

    def ffn(self, l, f):
        act, dve, sp = self.act, self.dve, self.sp
        i = 0 if f == 0 else 2
        self.norm(l, i)
        A, Bs, O = self.sl("ffn_a", f), self.sl("ffn_b", f), self.sl("ffn_o", f)
        gstore = []
        for pair in range(28):
            sa, la = self.load_slab(l, A[pair])
            sb_, lb = self.load_slab(l, Bs[pair])
            g = self.gst.take()
            for n, (c0, w) in enumerate(NTS):
                ba, ema = self.mm_fm(sa, la, KD, c0, w, self.bufA_ev)
                bb, emb = self.mm_fm(sb_, lb, KD, c0, w, self.bufA_ev)
                q = self.sil.take()
                ea = self.op(act, [ema] + self.sil.free[q], lambda: act.activation(out=self.sil.t[q][:, 0:w], in_=self.ps[:, ba, 0:w], func=AF.Silu))
                self.ps_release(ba, ea)
                ed = self.op(dve, [emb, ea] + (self.gst.free[g] if n == 0 else []),
                             lambda: dve.tensor_tensor(out=self.gst.t[g][:, c0:c0 + w], in0=self.sil.t[q][:, 0:w], in1=self.ps[:, bb, 0:w], op=ALU.mult))
                self.ps_release(bb, ed)
                self.sil.free[q] = [ed]
            self.wsl.free[sa] = [ema]
            self.wsl.free[sb_] = [emb]
            es_ = self.dma(sp, self.gT[:, pair, :], self.gst.t[g][:], self.gst.dsem[g], waits=[ed])
            self.gst.free[g] = [es_]
            gstore.append(es_)
        evs = []
        for piece in range(4):
            evs.append(self.mdma(sp, self.bufA[:, 7 * piece:7 * (piece + 1), :], self.gT[:, 7 * piece:7 * (piece + 1), :],
                                 waits=gstore + [emb]))
        self.bufA_ev = evs
        self.down_proj(l, O, KF, i, fuse_ss=not (f == 1 and l == self.nlayers - 1))

    def down_proj(self, l, O, kc, i, fuse_ss=True):
        dve, sp, act, pe = self.dve, self.sp, self.act, self.pe
        stores = []
        hl_q = {}
        ss_banks = [self.ps_take() for _ in range(3)] if fuse_ss else None
        pend = None

        def ss_mm(p, first, last):
            q, ea = p
            self.wait(pe, [ea])
            for n_, (c0_, w_) in enumerate(NTS):
                mm_ = pe.matmul(self.ps[:, ss_banks[n_], 0:w_], lhsT=self.ones_b[:], rhs=self.sqb.t[q][:, c0_:c0_ + w_], start=first, stop=last)
            em_ = self.ev(pe, mm_)
            self.sqb.free[q] = [em_]
            return em_

        def issue_h(c):
            hs = self.hch.take()
            hl_q[c] = (hs, self.dma(sp, self.hch.t[hs][:], self.hT[:, c, :], self.hch.dsem[hs], waits=self.hch.free[hs] + self.h_ev))

        issue_h(0)
        issue_h(1)
        for chunk, s in enumerate(O):
            slot, lev = self.load_slab(l, s)
            hs, hl = hl_q.pop(chunk)
            ho = self.hout.take()
            for n, (c0, w) in enumerate(NTS):
                bnk, em = self.mm_fm(slot, lev, kc, c0, w, self.bufA_ev)
                self.wait(dve, [em, hl, self.modAG_ev] + (self.hout.free[ho] if n == 0 else []))
                segs = [(c0, w, 0)] if c0 + w <= TL else ([(c0, w, 1)] if c0 >= TL else [(c0, TL - c0, 0), (TL, T - TL, 1)])
                for (a, ww, t) in segs:
                    ed = self.ev(dve, dve.scalar_tensor_tensor(out=self.hout.t[ho][:, a:a + ww], in0=self.ps[:, bnk, a - c0:a - c0 + ww],
                                                               scalar=self.modG[:, i, chunk:chunk + 1, t], in1=self.hch.t[hs][:, a:a + ww],
                                                               op0=ALU.mult, op1=ALU.add))
                self.ps_release(bnk, ed)
            self.wsl.free[slot] = [em]
            self.hch.free[hs] = [ed]
            if chunk + 2 < len(O):
                issue_h(chunk + 2)
            es_ = self.dma(sp, self.hT[:, chunk, :], self.hout.t[ho][:], self.hout.dsem[ho], waits=[ed])
            self.hout.free[ho] = [es_]
            stores.append(es_)
            if fuse_ss:
                q = self.sqb.take()
                ea = self.op(act, [ed] + self.sqb.free[q], lambda: act.activation(out=self.sqb.t[q][:], in_=self.hout.t[ho][:], func=AF.Square))
                self.hout.free[ho] = [es_, ea]
                if pend is not None:
                    ss_mm(pend, chunk == 1, False)
                pend = (q, ea)
        self.h_ev = stores
        self.bufA_free = [em]
        if fuse_ss:
            self.ss_ev = ss_mm(pend, False, True)
            self.bufA_free = [em, self.ss_ev]
            self.ss_banks = ss_banks
        self.modAG_free = self.modAG_free + [ed]

    def gelu_ps(self, psv, em, A, Bt, M=128):
        act, dve = self.act, self.dve
        e = self.op(act, [em], lambda: act.activation(out=A, in_=psv, func=AF.Square))
        e = self.op(dve, [e], lambda: dve.tensor_scalar(out=A, in0=A, scalar1=0.044715, scalar2=1.0, op0=ALU.mult, op1=ALU.add))
        e = self.op(dve, [e], lambda: dve.tensor_tensor(out=Bt, in0=A, in1=psv, op=ALU.mult))
        e = self.op(act, [e], lambda: act.activation(out=Bt, in_=Bt, func=AF.Sigmoid, scale=1.5957691216057308))
        e = self.op(dve, [e], lambda: dve.tensor_tensor(out=A, in0=Bt, in1=psv, op=ALU.mult))
        return e

    def inproj(self, l):
        act, dve, pe, sp = self.act, self.dve, self.pe, self.sp
        self.norm(l, 1)
        xev = []
        self.qk_ev = []
        self.pvn_ev = []
        zev = self.bufA_ev
        def heads(lst):
          for (slab, is_k) in lst:
              head = slab["idx"]
              slot, lev = self.load_slab(l, slab)
              g = self.gst.take()
              for n, (c0, w) in enumerate(NTS):
                  bnk, em = self.mm_fm(slot, lev, KD, c0, w, zev)
                  hs = self.hch.take()
                  csv = self.hch.t[hs][:, 0:2 * w].rearrange("p (a b) -> p a b", a=2)
                  el = self.dma(sp, csv, self.cs[:, :, c0:c0 + w], self.hch.dsem[hs], waits=self.hch.free[hs])
                  q1 = self.sqb.take()
                  ea = self.op(act, [em] + self.sqb.free[q1], lambda: act.activation(out=self.sqb.t[q1][:, 0:w], in_=self.ps[:, bnk, 0:w], func=AF.Square))
                  b2 = self.ps_take()
                  em2 = self.op(pe, [ea], lambda: pe.matmul(self.ps[:, b2, 0:w], lhsT=self.ones_b[:], rhs=self.sqb.t[q1][:, 0:w], start=True, stop=True))
                  self.sqb.free[q1] = [em2]
                  tq = self.tmpf.take()
                  tm = self.tmpf.t[tq]
                  ea2 = self.op(act, [em2] + self.tmpf.free[tq], lambda: act.activation(out=tm[:, 0:w], in_=self.ps[:, b2, 0:w], func=AF.Sqrt,
                                                                                        scale=1.0 / HD, bias=self.eps_t[:, 0:1]))
                  self.ps_release(b2, ea2)
                  e = self.op(dve, [ea2], lambda: dve.reciprocal(out=tm[:, 0:w], in_=tm[:, 0:w]))
                  e = self.op(dve, [e], lambda: dve.scalar_tensor_tensor(out=tm[:, NW:NW + w], in0=self.ps[:, bnk, 0:w], scalar=self.qkg[:, l, is_k:is_k + 1],
                                                                         in1=tm[:, 0:w], op0=ALU.mult, op1=ALU.mult))
                  self.ps_release(bnk, e)
                  q2 = self.sqb.take()
                  e3 = self.op(dve, [e] + self.sqb.free[q2], lambda: dve.tensor_copy(out=self.sqb.t[q2][:, 0:w], in_=tm[:, NW:NW + w]))
                  b3 = self.ps_take()
                  em3 = self.op(pe, [e3], lambda: pe.matmul(self.ps[:, b3, 0:w], lhsT=self.rot_b[:], rhs=self.sqb.t[q2][:, 0:w], start=True, stop=True))
                  self.sqb.free[q2] = [em3]
                  q3 = self.sil.take()
                  t3 = self.sil.t[q3]
                  e4 = self.op(dve, [el, em3, e3] + self.sil.free[q3], lambda: dve.tensor_tensor(out=t3[:, 0:w], in0=self.ps[:, b3, 0:w], in1=csv[:, 1, :], op=ALU.mult))
                  self.ps_release(b3, e4)
                  e5 = self.op(dve, [e4], lambda: dve.tensor_tensor(out=tm[:, NW:NW + w], in0=tm[:, NW:NW + w], in1=csv[:, 0, :], op=ALU.mult))
                  e6 = self.op(dve, [e5] + (self.gst.free[g] if n == 0 else []),
                               lambda: dve.tensor_tensor(out=self.gst.t[g][:, c0:c0 + w], in0=tm[:, NW:NW + w], in1=t3[:, 0:w], op=ALU.add))
                  self.hch.free[hs] = [e5]
                  self.tmpf.free[tq] = [e6]
                  self.sil.free[q3] = [e6]
              self.wsl.free[slot] = [em]
              gt = self.gst.t[g]
              if is_k:
                  es_ = self.dma(sp, self.Pk[:, head, :], gt[:], self.gst.dsem[g], waits=[e6])
                  base = XK + head * 288
                  self.dma(sp, self.xin[0][:, base:base + 128], gt[:, 0:128], self.gst.dsem[g])
                  self.dma(sp, self.xin[0][:, base + 128:base + 256], gt[:, 896:1024], self.gst.dsem[g])
                  es_ = self.dma(sp, self.xin[0][:, base + 256:base + 288], gt[:, 1024:1056], self.gst.dsem[g])
                  xev.append(es_)
              else:
                  es_ = self.dma(sp, self.Pq[:, head, :], gt[:], self.gst.dsem[g], waits=[e6])
              self.gst.free[g] = [es_]
              self.qk_ev = getattr(self, "qk_ev", []) + [es_]
        heads([(s, 1) for s in self.sl("k")])
        for slab in self.sl("v"):
            hk = slab["idx"]
            slot, lev = self.load_slab(l, slab)
            for tb in range(9):
                bnk, em, M = self.mm_tm(slot, lev, tb, zev)
                q1 = self.sqb.take()
                st = self.sqb.t[q1]
                ea = self.op(act, [em] + self.sqb.free[q1], lambda: act.activation(out=st[0:M, 0:128], in_=self.ps[0:M, bnk, 0:128], func=AF.Identity))
                self.ps_release(bnk, ea)
                es_ = self.dma(sp, self.Pv[128 * tb:128 * tb + M, 128 * hk:128 * hk + 128], st[0:M, 0:128], self.sqb.dsem[q1], waits=[ea])
                if tb == 0:
                    es_ = self.dma(sp, self.xin[0][:, XV + 128 * hk:XV + 128 * hk + 128], st[:, 0:128], self.sqb.dsem[q1])
                if tb == 7:
                    es_ = self.dma(sp, self.xin[0][:, XV + 256 + 128 * hk:XV + 256 + 128 * hk + 128], st[:, 0:128], self.sqb.dsem[q1])
                if tb == 8:
                    es_ = self.dma(sp, self.xin[0][0:TCX, XVC + 128 * hk:XVC + 128 * hk + 128], st[0:TCX, 0:128], self.sqb.dsem[q1])
                self.sqb.free[q1] = [es_]
                xev.append(es_)
            self.wsl.free[slot] = [em]
        heads([(s, 0) for s in self.sl("q")])
        fs = self.sl("f")
        for grp in range(4):
            gsl = []
            fev = []
            for c in range(2):
                slab = fs[2 * grp + c]
                slot, lev = self.load_slab(l, slab)
                g = self.gst.take()
                gsl.append(g)
                for n, (c0, w) in enumerate(NTS):
                    bnk, em = self.mm_fm(slot, lev, KD, c0, w, zev)
                    ea = self.op(act, [em] + (self.gst.free[g] if n == 0 else []),
                                 lambda: act.activation(out=self.gst.t[g][:, c0:c0 + w], in_=self.ps[:, bnk, 0:w], func=AF.Identity))
                    self.ps_release(bnk, ea)
                self.wsl.free[slot] = [em]
                fev.append(ea)
            for tb in range(9):
                M = 128 if tb < 8 else TCX
                bnk = self.ps_take()
                self.wait(pe, fev)
                for c in range(2):
                    mm = pe.matmul(self.ps[0:M, bnk, 0:512], lhsT=self.gst.t[gsl[c]][:, 128 * tb:128 * tb + M], rhs=self.cs256_t[:, c, :],
                                   start=(c == 0), stop=(c == 1))
                em = self.ev(pe, mm)
                q1 = self.sqb.take()
                st = self.sqb.t[q1]
                ed = self.op(dve, [em] + self.sqb.free[q1], lambda: dve.tensor_copy(out=st[0:M, 0:512], in_=self.ps[0:M, bnk, 0:512]))
                self.ps_release(bnk, ed)
                col = (tb % 2) * 2048 + grp * 512
                es_ = self.dma(sp, self.xin[1 + tb // 2][0:M, col:col + 512], st[0:M, 0:512], self.sqb.dsem[q1], waits=[ed])
                self.sqb.free[q1] = [es_]
                xev.append(es_)
            for g in gsl:
                self.gst.free[g] = [em]
        for slab in self.sl("sv"):
            grp = slab["idx"]
            slot, lev = self.load_slab(l, slab)
            hs = self.hch.take()
            lt = self.hch.t[hs]
            self.dma(sp, lt[:, 0:128], self.lng[0:1, l * 1024 + 128 * grp:l * 1024 + 128 * grp + 128].partition_broadcast(128),
                     self.hch.dsem[hs], waits=self.hch.free[hs])
            el = self.dma(sp, lt[:, 128:256], self.lnb[0:1, l * 1024 + 128 * grp:l * 1024 + 128 * grp + 128].partition_broadcast(128),
                          self.hch.dsem[hs])
            for tb in range(9):
                bnk, em, M = self.mm_tm(slot, lev, tb, zev)
                tq = self.tmpf.take()
                tm = self.tmpf.t[tq]
                A, Bt, st = tm[0:M, 0:128], tm[0:M, 128:256], tm[0:M, 256:272]
                psv = self.ps[0:M, bnk, 0:128]
                self.wait(act, self.tmpf.free[tq])
                e = self.gelu_ps(psv, em, A, Bt, M)
                self.ps_release(bnk, e)
                e1 = self.op(dve, [e], lambda: dve.tensor_reduce(out=st[:, 0:1], in_=A, axis=AX.X, op=ALU.add))
                e2 = self.op(act, [e], lambda: act.activation(out=Bt, in_=A, func=AF.Square, accum_out=st[:, 1:2]))
                e = self.op(dve, [e1], lambda: dve.tensor_scalar(out=st[:, 2:3], in0=st[:, 0:1], scalar1=1.0 / 128, scalar2=None, op0=ALU.mult))
                e = self.op(dve, [e], lambda: dve.tensor_tensor(out=st[:, 3:4], in0=st[:, 2:3], in1=st[:, 2:3], op=ALU.mult))
                e = self.op(dve, [e, e2], lambda: dve.scalar_tensor_tensor(out=st[:, 4:5], in0=st[:, 1:2], scalar=1.0 / 128, in1=st[:, 3:4],
                                                                           op0=ALU.mult, op1=ALU.subtract))
                e = self.op(act, [e], lambda: act.activation(out=st[:, 5:6], in_=st[:, 4:5], func=AF.Sqrt, scale=1.0, bias=self.eps_t[0:M, 0:1]))
                e = self.op(dve, [e], lambda: dve.reciprocal(out=st[:, 6:7], in_=st[:, 5:6]))
                e = self.op(dve, [e], lambda: dve.tensor_scalar(out=Bt, in0=A, scalar1=st[:, 2:3], scalar2=st[:, 6:7], op0=ALU.subtract, op1=ALU.mult))
                e = self.op(dve, [e, el], lambda: dve.tensor_tensor(out=Bt, in0=Bt, in1=lt[0:M, 0:128], op=ALU.mult))
                q1 = self.sqb.take()
                so = self.sqb.t[q1]
                e = self.op(dve, [e] + self.sqb.free[q1], lambda: dve.tensor_tensor(out=so[0:M, 0:128], in0=Bt, in1=lt[0:M, 128:256], op=ALU.add))
                self.tmpf.free[tq] = [e]
                es_ = self.dma(sp, self.Pvn[128 * tb:128 * tb + M, 128 * grp:128 * grp + 128], so[0:M, 0:128], self.sqb.dsem[q1], waits=[e])
                if tb == 8:
                    es_ = self.dma(sp, self.xin[0][0:TCX, XVN + 128 * grp:XVN + 128 * grp + 128], so[0:TCX, 0:128], self.sqb.dsem[q1])
                    xev.append(es_)
                self.sqb.free[q1] = [es_]
                self.pvn_ev = getattr(self, "pvn_ev", []) + [es_]
            self.wsl.free[slot] = [em]
            self.hch.free[hs] = [e]
        self.pu_ev = []
        for slab in self.sl("su"):
            grp = slab["idx"]
            slot, lev = self.load_slab(l, slab)
            g = self.gst.take()
            for n, (c0, w) in enumerate(NTS):
                bnk, em = self.mm_fm(slot, lev, KD, c0, w, zev)
                tq = self.tmpf.take()
                tm = self.tmpf.t[tq]
                self.wait(act, self.tmpf.free[tq])
                e = self.gelu_ps(self.ps[:, bnk, 0:w], em, tm[:, 0:w], tm[:, NW:NW + w])
                self.ps_release(bnk, e)
                e = self.op(act, [e] + (self.gst.free[g] if n == 0 else []),
                            lambda: act.activation(out=self.gst.t[g][:, c0:c0 + w], in_=tm[:, 0:w], func=AF.Identity))
                self.tmpf.free[tq] = [e]
            self.wsl.free[slot] = [em]
            es_ = self.dma(sp, self.Pu[:, grp, :], self.gst.t[g][:], self.gst.dsem[g], waits=[e])
            self.gst.free[g] = [es_]
            self.pu_ev.append(es_)
        self.pcx_ev = []
        eedge = []
        for kind, nm in enumerate(("cx", "cc", "cb")):
            pass
        cxs, ccs, cbs = self.sl("cx"), self.sl("cc"), self.sl("cb")
        for i in range(8):
            for kind, slab in enumerate((cxs[i], ccs[i], cbs[i])):
                slot, lev = self.load_slab(l, slab)
                ho = self.hout.take()
                for n, (c0, w) in enumerate(NTS):
                    bnk, em = self.mm_fm(slot, lev, KD, c0, w, zev)
                    ea = self.op(act, [em] + (self.hout.free[ho] if n == 0 else []),
                                 lambda: act.activation(out=self.hout.t[ho][:, c0:c0 + w], in_=self.ps[:, bnk, 0:w], func=AF.Identity))
                    self.ps_release(bnk, ea)
                self.wsl.free[slot] = [em]
                fr = []
                if kind < 2:
                    et = self.xe if kind == 0 else self.ce
                    ht = self.hout.t[ho]
                    self.wait(dve, [ea])
                    dve.tensor_copy(out=et[:, i, 0:1], in_=ht[:, 0:1])
                    dve.tensor_copy(out=et[:, i, 1:2], in_=ht[:, 1023:1024])
                    dve.tensor_copy(out=et[:, i, 2:3], in_=ht[:, 1024:1025])
                    ee = self.ev(dve, dve.tensor_copy(out=et[:, i, 3:4], in_=ht[:, 1055:1056]))
                    fr.append(ee)
                    eedge.append(ee)
                es_ = self.dma(sp, self.Pcx[:, kind, i, :], self.hout.t[ho][:], self.hout.dsem[ho], waits=[ea])
                self.hout.free[ho] = [es_] + fr
                self.pcx_ev.append(es_)
        ez = self.op(dve, eedge, lambda: dve.tensor_tensor(out=self.zeo[:], in0=self.xe[:].rearrange("p a b -> p (a b)"),
                                                           in1=self.ce[:].rearrange("p a b -> p (a b)"), op=ALU.mult))
        ezs = self.mdma(sp, self.xinz[:, :], self.zeo[:], waits=[ez])
        self.exchange(xev, ezs)
        self.pgate_ev = []
        for slab in self.sl("gate"):
            ch, r = slab["idx"], slab["r"]
            slot, lev = self.load_slab(l, slab)
            g = self.gst.take()
            for n, (c0, w) in enumerate(NTS):
                bnk, em = self.mm_fm(slot, lev, KD, c0, w, zev)
                ea = self.op(act, [em] + (self.gst.free[g] if n == 0 else []),
                             lambda: act.activation(out=self.gst.t[g][:, c0:c0 + w], in_=self.ps[:, bnk, 0:w], func=AF.Sigmoid))
                self.ps_release(bnk, ea)
            self.wsl.free[slot] = [em]
            es_ = self.dma(sp, self.Pgate[:, r, ch, :], self.gst.t[g][:], self.gst.dsem[g], waits=[ea])
            self.gst.free[g] = [es_]
            self.pgate_ev.append(es_)
        self.bufA_free = [em]
        self.arena_ok = [em]
        return xev, ezs

    def exchange(self, xev, ezs):
        self.wait(self.pool, xev + [ezs])
        for u in range(XU):
            self.gather(self.xin[u][:, :], self.xall[u], BF16)
        self.gather(self.xinz[:, :], self.zall, F32)
        self.x_ev = [self.gather_flush()]

    def select(self, out, cand_fn, which, waits):
        dve = self.dve
        e = self.op(dve, waits, lambda: dve.tensor_scalar(out=out, in0=cand_fn(0), scalar1=self.oh_t[:, which, 0:1], scalar2=None, op0=ALU.mult))
        for a in range(1, 8):
            e = self.op(dve, [e], lambda: dve.scalar_tensor_tensor(out=out, in0=cand_fn(a), scalar=self.oh_t[:, which, a:a + 1], in1=out,
                                                                   op0=ALU.mult, op1=ALU.add))
        return e

    def branches(self, l):
        act, dve, pe, sp = self.act, self.dve, self.pe, self.sp
        ok = self.arena_ok + self.x_ev
        ys_ev = []
        used = []
        scale = float(HD) ** -0.5
        masks = self.pg(6, 2)[:, 0:2048].rearrange("p (a b) -> p a b", a=4)
        emask = self.mdma(sp, masks, self.masks[:], waits=ok)
        E_pages = [self.pg(8)[:, 0:512], self.pg(8)[:, 512:1024], self.pg(9)[:, 0:512], self.pg(9)[:, 512:1024]]
        E_free = [[] for _ in range(4)]
        E_i = 0
        qt_pages = [self.pg(10)[:, 0:512], self.pg(10)[:, 512:1024], self.pg(11)[:, 0:512]]
        qt_free = [[] for _ in range(3)]
        qt_sem = [self.sem(f"qt{l}_{i}") for i in range(3)] if not hasattr(self, "qt_sem") else self.qt_sem
        self.qt_sem = qt_sem
        qt_i = 0
        den_t = [self.pg(12, 1, F32)[:, 0:512], self.pg(13, 1, F32)[:, 0:512]]
        yst = [self.pg(14)[:, 0:512], self.pg(15)[:, 0:512]]
        yst_free = [[], []]
        yst_sem = [self.sem(f"yst{i}") for i in range(2)] if not hasattr(self, "yst_sem") else self.yst_sem
        self.yst_sem = yst_sem
        y_i = 0
        for hk in range(2):
            kt = self.pg(0, 2)
            vt = self.pg(2, 2)[:, 0:1536].rearrange("p (a b) -> p a b", a=12)
            candK = self.pg(4)[:, 0:1024].rearrange("p (a b) -> p a b", a=8)
            candK2 = self.pg(5)[:, 0:1024].rearrange("p (a b) -> p a b", a=8)
            candV = self.pg(16)[:, 0:1024].rearrange("p (a b) -> p a b", a=8)
            candV2 = self.pg(17)[:, 0:1024].rearrange("p (a b) -> p a b", a=8)
            w0 = ok + used
            lev = [self.mdma(sp, kt[:, 128:1152], self.Pk[:, hk, 0:TL], waits=w0 + self.qk_ev)]
            base = XK + hk * 288
            for r in range(8):
                lev.append(self.mdma(sp, kt[:, 1280 + 32 * r:1280 + 32 * r + 32], self.xall[0][pos_of_rank(r), :, base + 256:base + 288], waits=w0))
            lev.append(self.mdma(sp, candK, self.xall[0][:, :, base + 128:base + 256].rearrange("a p x -> p a x"), waits=w0))
            lev.append(self.mdma(sp, candK2, self.xall[0][:, :, base:base + 128].rearrange("a p x -> p a x"), waits=w0))
            lev.append(self.mdma(sp, vt[:, 1:9, :], self.Pv[0:TL, 128 * hk:128 * hk + 128].rearrange("(b p) c -> p b c", p=128), waits=w0))
            for r in range(8):
                lev.append(self.mdma(sp, vt[32 * (r % 4):32 * (r % 4) + 32, 10 + r // 4, :],
                                     self.xall[0][pos_of_rank(r), 0:TCX, XVC + 128 * hk:XVC + 128 * hk + 128], waits=w0))
            lev.append(self.mdma(sp, candV, self.xall[0][:, :, XV + 256 + 128 * hk:XV + 256 + 128 * hk + 128].rearrange("a p x -> p a x"), waits=w0))
            lev.append(self.mdma(sp, candV2, self.xall[0][:, :, XV + 128 * hk:XV + 128 * hk + 128].rearrange("a p x -> p a x"), waits=w0))
            s1 = self.select(kt[:, 0:128], lambda a: candK[:, a, :], 0, lev)
            s2 = self.select(kt[:, 1152:1280], lambda a: candK2[:, a, :], 1, [s1])
            s3 = self.select(vt[:, 0, :], lambda a: candV[:, a, :], 0, [s2])
            s4 = self.select(vt[:, 9, :], lambda a: candV2[:, a, :], 1, [s3])
            kv_ready = lev + [s4, emask]
            last_pe = None
            for qb in range(9):
                nq = 128 if qb < 8 else TCX
                N = 4 * nq
                c0 = 128 * qb
                qs = qt_i % 3
                qt_i += 1
                qt = qt_pages[qs][:, 0:N].rearrange("p (a b) -> p a b", a=4)
                eq = self.dma(sp, qt, self.Pq[:, 4 * hk:4 * hk + 4, c0:c0 + nq], qt_sem[qs], waits=qt_free[qs] + w0 + self.qk_ev)
                if qb < 8:
                    kbl = [(qb * 128, qb, 2 if qb == 0 else 0), ((qb + 1) * 128, qb + 1, None), ((qb + 2) * 128, qb + 2, 3 if qb == 7 else 1),
                           (1280, 10, None), (1408, 11, None)]
                else:
                    kbl = [(1280, 10, None), (1408, 11, None)]
                bO = self.ps_take()
                bD = self.ps_take()
                Es = []

                def s_mm(j):
                    kc0, vb, mk = kbl[j]
                    bS = self.ps_take()
                    em = self.op(pe, kv_ready + [eq], lambda: pe.matmul(self.ps[:, bS, 0:N], lhsT=kt[:, kc0:kc0 + 128], rhs=qt_pages[qs][:, 0:N], start=True, stop=True))
                    nonlocal E_i
                    es = E_i % 4
                    E_i += 1
                    Et = E_pages[es][:, 0:N]
                    ea = self.op(act, [em] + E_free[es], lambda: act.activation(out=Et, in_=self.ps[:, bS, 0:N], func=AF.Exp, scale=scale))
                    self.ps_release(bS, ea)
                    if mk is not None:
                        ea = self.op(dve, [ea], lambda: dve.tensor_tensor(out=Et, in0=Et, in1=masks[:, mk, 0:N], op=ALU.mult))
                    return es, Et, ea

                nxt = s_mm(0)
                for j in range(len(kbl)):
                    es, Et, ea = nxt
                    if j + 1 < len(kbl):
                        nxt = s_mm(j + 1)
                    vb = kbl[j][1]
                    self.wait(pe, [ea])
                    pe.matmul(self.ps[:, bO, 0:N], lhsT=vt[:, vb, :], rhs=Et, start=(j == 0), stop=(j == len(kbl) - 1))
                    em = self.ev(pe, pe.matmul(self.ps[:, bD, 0:N], lhsT=self.ones_b[:], rhs=Et, start=(j == 0), stop=(j == len(kbl) - 1)))
                    E_free[es] = [em]
                last_pe = em
                qt_free[qs] = [em]
                dt_ = den_t[y_i % 2][:, 0:N]
                self.wait(dve, [em])
                for gq in range(4):
                    e = self.ev(dve, dve.tensor_scalar(out=dt_[:, gq * nq:(gq + 1) * nq], in0=self.ps[:, bD, gq * nq:(gq + 1) * nq],
                                                       scalar1=self.esk[:, l * 8 + 4 * hk + gq:l * 8 + 4 * hk + gq + 1], scalar2=None, op0=ALU.add))
                self.ps_release(bD, e)
                e = self.op(dve, [e], lambda: dve.reciprocal(out=dt_, in_=dt_))
                ys = y_i % 2
                y_i += 1
                yt = yst[ys][:, 0:N]
                e = self.op(dve, [e] + yst_free[ys], lambda: dve.tensor_tensor(out=yt, in0=self.ps[:, bO, 0:N], in1=dt_, op=ALU.mult))
                self.ps_release(bO, e)
                es_ = self.dma(sp, self.Pys[:, 4 * hk:4 * hk + 4, c0:c0 + nq], yt.rearrange("p (a b) -> p a b", a=4), yst_sem[ys], waits=[e])
                yst_free[ys] = [es_]
                ys_ev.append(es_)
            used = [last_pe]
        w0 = ok + used
        wsTb = self.pg(18)[:, 0:1024].rearrange("p (a b) -> p a b", a=8)
        wscb = self.pg(19)[:, 0:512].rearrange("p (c a b) -> p c a b", c=2, a=8)
        bst2 = self.pg(20, 2, F32)[:, 0:1024]
        bsc2 = self.pg(22, 1, F32)[:, 0:256]
        bst = bst2.rearrange("p (a b) -> p a b", a=8)
        bsc = bsc2.rearrange("p (a b) -> p a b", a=8)
        vnctx = self.pg(23, 2)[:, 0:2048].rearrange("p (a b) -> p a b", a=2)
        hs = self.hch.take()
        el = self.dma(sp, self.hch.t[hs][:, 0:1024].rearrange("p (a b) -> p a b", a=8), self.wsT[:, l], self.hch.dsem[hs], waits=self.hch.free[hs])
        e1 = self.op(act, [el] + w0, lambda: act.activation(out=wsTb, in_=self.hch.t[hs][:, 0:1024].rearrange("p (a b) -> p a b", a=8), func=AF.Identity))
        self.hch.free[hs] = [e1]
        hs = self.hch.take()
        el = self.dma(sp, self.hch.t[hs][:, 0:512].rearrange("p (c a b) -> p c a b", c=2, a=8), self.wsc[:, l], self.hch.dsem[hs], waits=self.hch.free[hs])
        e2 = self.op(act, [el] + w0, lambda: act.activation(out=wscb, in_=self.hch.t[hs][:, 0:512].rearrange("p (c a b) -> p c a b", c=2, a=8), func=AF.Identity))
        self.hch.free[hs] = [e2]
        sg = [e1, e2]
        sg.append(self.mdma(sp, bst2, self.bsT[0:1, l * 1024:(l + 1) * 1024].partition_broadcast(128), waits=w0))
        sg.append(self.mdma(sp, bsc2, self.bsc[0:1, l * 256:(l + 1) * 256].partition_broadcast(128), waits=w0))
        for r in range(8):
            sg.append(self.mdma(sp, vnctx[32 * (r % 4):32 * (r % 4) + 32, r // 4, :], self.xall[0][pos_of_rank(r), 0:TCX, XVN:XVN + 1024], waits=w0))
        vnl = [self.pg(25)[:, 0:1024].rearrange("p (a b) -> p a b", a=8), self.pg(26)[:, 0:1024].rearrange("p (a b) -> p a b", a=8)]
        ut = [self.pg(27), self.pg(28)]
        yo = [self.pg(29), self.pg(30)]
        slot_free = [[], []]
        for grp in range(8):
            sl_ = grp % 2
            lv = self.mdma(sp, vnl[sl_], self.Pvn[0:TL, 128 * grp:128 * grp + 128].rearrange("(b p) c -> p b c", p=128), waits=w0 + self.pvn_ev + slot_free[sl_])
            lu = self.mdma(sp, ut[sl_], self.Pu[:, grp, :], waits=w0 + self.pu_ev + slot_free[sl_])
            e = None
            for half in range(2):
                bnk = self.ps_take()
                self.wait(pe, [lv] + sg)
                for t4 in range(4):
                    tb = 4 * half + t4
                    mm = pe.matmul(self.ps[:, bnk, 128 * t4:128 * t4 + 128], lhsT=vnl[sl_][:, tb, :], rhs=wsTb[:, grp, :], start=True, stop=True)
                em = self.ev(pe, mm)
                tq = self.tmpf.take()
                tm = self.tmpf.t[tq]
                self.wait(dve, [em, lu] + sg + self.tmpf.free[tq] + slot_free[sl_])
                for t4 in range(4):
                    tb = 4 * half + t4
                    e = self.ev(dve, dve.tensor_tensor(out=tm[:, 128 * t4:128 * t4 + 128], in0=self.ps[:, bnk, 128 * t4:128 * t4 + 128], in1=bst[:, grp, :], op=ALU.add))
                self.ps_release(bnk, e)
                e = self.op(dve, [e], lambda: dve.tensor_tensor(out=yo[sl_][:, 512 * half:512 * half + 512], in0=tm[:, 0:512],
                                                                in1=ut[sl_][:, 512 * half:512 * half + 512], op=ALU.mult))
                self.tmpf.free[tq] = [e]
            bnk = self.ps_take()
            self.wait(pe, sg)
            for ch in range(2):
                mm = pe.matmul(self.ps[:, bnk, 0:TCX], lhsT=vnctx[:, ch, 128 * grp:128 * grp + 128], rhs=wscb[:, ch, grp, :], start=(ch == 0), stop=(ch == 1))
            em = self.ev(pe, mm)
            tq = self.tmpf.take()
            tm = self.tmpf.t[tq]
            e = self.op(dve, [em] + self.tmpf.free[tq], lambda: dve.tensor_tensor(out=tm[:, 0:TCX], in0=self.ps[:, bnk, 0:TCX], in1=bsc[:, grp, :], op=ALU.add))
            self.ps_release(bnk, e)
            e = self.op(dve, [e], lambda: dve.tensor_tensor(out=yo[sl_][:, TL:T], in0=tm[:, 0:TCX], in1=ut[sl_][:, TL:T], op=ALU.mult))
            self.tmpf.free[tq] = [e]
            es_ = self.mdma(sp, self.Pys[:, 8 + grp, :], yo[sl_], waits=[e])
            slot_free[sl_] = [es_, em]
            ys_ev.append(es_)
        zc = self.pg(31, 1, F32)[:, 0:256].rearrange("p (a b) -> p a b", a=8)
        lz = self.mdma(sp, zc, self.zall[:, :, :].rearrange("a p x -> p a x"), waits=w0)
        s1 = self.select(self.selz[:, 0, :], lambda a: zc[:, a, :], 0, [lz])
        s2 = self.select(self.selz[:, 1, :], lambda a: zc[:, a, :], 1, [s1])
        zt = self.pg(3, 3, F32)
        for i in range(8):
            hx, hc_, hb = self.hch.take(), self.hch.take(), self.hch.take()
            lx = self.dma(sp, self.hch.t[hx][:], self.Pcx[:, 0, i, :], self.hch.dsem[hx], waits=self.hch.free[hx] + self.pcx_ev)
            lc = self.dma(sp, self.hch.t[hc_][:], self.Pcx[:, 1, i, :], self.hch.dsem[hc_], waits=self.hch.free[hc_] + self.pcx_ev)
            lb = self.dma(sp, self.hch.t[hb][:], self.Pcx[:, 2, i, :], self.hch.dsem[hb], waits=self.hch.free[hb] + self.pcx_ev)
            X, C_, B_ = self.hch.t[hx], self.hch.t[hc_], self.hch.t[hb]
            ho = self.hout.take()
            yt = self.hout.t[ho]
            e = self.op(dve, [lx, lc, s2] + w0 + used + getattr(self, "zt_free", []), lambda: dve.tensor_tensor(out=zt[:, 1:1025], in0=X[:, 0:TL], in1=C_[:, 0:TL], op=ALU.mult))
            e = self.op(dve, [e], lambda: dve.tensor_tensor(out=zt[:, 1027:1059], in0=X[:, TL:T], in1=C_[:, TL:T], op=ALU.mult))
            dve.tensor_copy(out=zt[:, 0:1], in_=self.selz[:, 0, 4 * i + 1:4 * i + 2])
            dve.tensor_copy(out=zt[:, 1025:1026], in_=self.selz[:, 1, 4 * i + 0:4 * i + 1])
            dve.tensor_copy(out=zt[:, 1026:1027], in_=self.selz[:, 0, 4 * i + 3:4 * i + 4])
            e = self.ev(dve, dve.tensor_copy(out=zt[:, 1059:1060], in_=self.selz[:, 1, 4 * i + 2:4 * i + 3]))
            self.hch.free[hx] = [e]
            self.hch.free[hc_] = [e]
            q1 = self.sqb.take()
            so = self.sqb.t[q1]
            self.wait(dve, [e] + self.hout.free[ho])
            for (o, n_, z0) in ((0, TL, 1), (TL, TCX, 1027)):
                e = self.ev(dve, dve.tensor_scalar(out=yt[:, o:o + n_], in0=zt[:, z0:z0 + n_], scalar1=self.cw_t[:, l, i, 1:2], scalar2=None, op0=ALU.mult))
                self.wait(dve, [e])
                e = self.ev(dve, dve.scalar_tensor_tensor(out=yt[:, o:o + n_], in0=zt[:, z0 - 1:z0 - 1 + n_], scalar=self.cw_t[:, l, i, 0:1], in1=yt[:, o:o + n_],
                                                          op0=ALU.mult, op1=ALU.add))
                self.wait(dve, [e])
                e = self.ev(dve, dve.scalar_tensor_tensor(out=yt[:, o:o + n_], in0=zt[:, z0 + 1:z0 + 1 + n_], scalar=self.cw_t[:, l, i, 2:3], in1=yt[:, o:o + n_],
                                                          op0=ALU.mult, op1=ALU.add))
                self.wait(dve, [e])
            self.zt_free = [e]
            e = self.op(dve, [e, lb] + self.sqb.free[q1], lambda: dve.tensor_tensor(out=so[:], in0=yt[:], in1=B_[:], op=ALU.mult))
            self.hch.free[hb] = [e]
            self.hout.free[ho] = [e]
            es_ = self.dma(sp, self.Pys[:, 16 + i, :], so[:], self.sqb.dsem[q1], waits=[e])
            self.sqb.free[q1] = [es_]
            ys_ev.append(es_)
        zctx = self.pg(6, 4)[:, 0:4096].rearrange("p (a b) -> p a b", a=2)
        lzc = []
        for r in range(8):
            lzc.append(self.mdma(sp, zctx[32 * (r % 4):32 * (r % 4) + 32, r // 4, :], self.xall[5][pos_of_rank(r), 0:TCX, 0:2048], waits=w0 + used))
        prior = list(ys_ev)
        zp = [self.pg(p0, 2)[:, 0:2048].rearrange("p (b g c x) -> p b g c x", b=2, g=2, c=2) for p0 in (0, 10, 12)]
        zp_sem = [self.sem(f"zp{i}") for i in range(3)] if not hasattr(self, "zp_sem") else self.zp_sem
        self.zp_sem = zp_sem
        zp_free = [[], [], []]
        zp_i = 0
        fo = [self.pg(14), self.pg(15)]
        fo_free = [[], []]
        for gp in range(2):
            banks = [self.ps_take() for _ in range(8)]
            for tp in range(32):
                slot = self.wsl.take()
                ct = self.wsl.t[slot][:].rearrange("p a b -> p (a b)").rearrange("p (b c x) -> p b c x", b=2, c=2)
                lc = self.dma(sp, ct, self.cts[:, 2 * tp:2 * tp + 2, :, :], self.wsl.dsem[slot], waits=self.wsl.free[slot])
                zs = zp_i % 3
                zp_i += 1
                first_d = True
                for bi in range(2):
                    b_ = 2 * tp + bi
                    r, tb = b_ // 8, b_ % 8
                    for gi in range(2):
                        col = (tb % 2) * 2048 + (2 * gp + gi) * 512
                        lzp = self.dma(sp, zp[zs][:, bi, gi, :, :], self.xall[1 + tb // 2][pos_of_rank(r), :, col:col + 512].rearrange("p (c x) -> p c x", c=2),
                                       zp_sem[zs], waits=(zp_free[zs] + w0 + used + prior) if first_d else None)
                        first_d = False
                self.wait(pe, [lc, lzp])
                for bi in range(2):
                    for c in range(2):
                        first = (tp == 0 and bi == 0 and c == 0)
                        last = (tp == 31 and bi == 1 and c == 1)
                        for gi in range(2):
                            for half in range(2):
                                for n2 in range(2):
                                    mm = pe.matmul(self.ps[:, banks[4 * gi + 2 * half + n2], 0:512], lhsT=zp[zs][:, bi, gi, c, 128 * half:128 * half + 128],
                                                   rhs=ct[:, bi, c, 512 * n2:512 * n2 + 512], start=first, stop=last)
                em = self.ev(pe, mm)
                self.wsl.free[slot] = [em]
                zp_free[zs] = [em]
            for gi in range(2):
                grp = 2 * gp + gi
                for half in range(2):
                    fs_ = half
                    self.wait(act, [em] + fo_free[fs_] + w0 + used + prior)
                    for n2 in range(2):
                        ea = self.ev(act, act.activation(out=fo[fs_][:, 512 * n2:512 * n2 + 512], in_=self.ps[:, banks[4 * gi + 2 * half + n2], 0:512], func=AF.Identity))
                        self.ps_release(banks[4 * gi + 2 * half + n2], ea)
                    bnk = self.ps_take()
                    self.wait(pe, lzc)
                    for kb in range(2):
                        for c in range(2):
                            mm = pe.matmul(self.ps[:, bnk, 0:TCX], lhsT=zctx[:, kb, grp * 512 + c * 256 + half * 128:grp * 512 + c * 256 + half * 128 + 128],
                                           rhs=self.ct256_t[:, kb, c, :], start=(kb == 0 and c == 0), stop=(kb == 1 and c == 1))
                    emc = self.ev(pe, mm)
                    ea = self.op(act, [emc], lambda: act.activation(out=fo[fs_][:, TL:T], in_=self.ps[:, bnk, 0:TCX], func=AF.Identity))
                    self.ps_release(bnk, ea)
                    es_ = self.mdma(sp, self.Pys[:, 24 + 2 * grp + half, :], fo[fs_], waits=[ea])
                    fo_free[fs_] = [es_]
                    ys_ev.append(es_)
        self.ys_ev = ys_ev
        self.arena_done = [em, emc, ea]

    def merge(self, l):
        act, dve, sp = self.act, self.dve, self.sp
        evs = []
        for piece in range(4):
            evs.append(self.mdma(sp, self.bufA[:, 8 * piece:8 * (piece + 1), :], self.Pys[:, 8 * piece:8 * (piece + 1), :],
                                 waits=self.ys_ev + self.arena_done))
        self.bufA_ev = evs
        self.pf_limit = len(STREAM) - 1
        br = self.sl("br")
        mstores = []
        gl_q = {}

        def issue_g(idx):
            q1 = self.sqb.take()
            gl_q[idx] = (q1, self.dma(sp, self.sqb.t[q1][:], self.Pgate[:, idx % 4, idx // 4, :], self.sqb.dsem[q1],
                                      waits=self.sqb.free[q1] + self.pgate_ev))

        issue_g(0)
        issue_g(1)
        for ch in range(32):
            ho = self.hout.take()
            acc = self.hout.t[ho]
            for r in range(4):
                slab = br[4 * ch + r]
                slot, lev = self.load_slab(l, slab)
                q1, gl = gl_q.pop(4 * ch + r)
                for n, (c0, w) in enumerate(NTS):
                    bnk, em = self.mm_fm(slot, lev, 8, c0, w, self.bufA_ev, rhs0=8 * r)
                    if r == 0:
                        e = self.op(dve, [em, gl] + (self.hout.free[ho] if n == 0 else []),
                                    lambda: dve.tensor_tensor(out=acc[:, c0:c0 + w], in0=self.ps[:, bnk, 0:w], in1=self.sqb.t[q1][:, c0:c0 + w], op=ALU.mult))
                        self.ps_release(bnk, e)
                    else:
                        tq = self.sil.take()
                        e = self.op(dve, [em, gl] + self.sil.free[tq],
                                    lambda: dve.tensor_tensor(out=self.sil.t[tq][:, 0:w], in0=self.ps[:, bnk, 0:w], in1=self.sqb.t[q1][:, c0:c0 + w], op=ALU.mult))
                        self.ps_release(bnk, e)
                        e = self.op(dve, [e], lambda: dve.tensor_tensor(out=acc[:, c0:c0 + w], in0=acc[:, c0:c0 + w], in1=self.sil.t[tq][:, 0:w], op=ALU.add))
                        self.sil.free[tq] = [e]
                self.wsl.free[slot] = [em]
                self.sqb.free[q1] = [e]
                if 4 * ch + r + 2 < 128:
                    issue_g(4 * ch + r + 2)
            g = self.gst.take()
            ea = self.op(act, [e] + self.gst.free[g], lambda: act.activation(out=self.gst.t[g][:], in_=acc[:], func=AF.Identity))
            self.hout.free[ho] = [ea]
            es_ = self.dma(sp, self.mT[:, ch, :], self.gst.t[g][:], self.gst.dsem[g], waits=[ea])
            self.gst.free[g] = [es_]
            mstores.append(es_)
        evs = []
        for piece in range(4):
            evs.append(self.mdma(sp, self.bufA[:, 8 * piece:8 * (piece + 1), :], self.mT[:, 8 * piece:8 * (piece + 1), :], waits=mstores + [em]))
        self.bufA_ev = evs
        self.down_proj(l, self.sl("wo"), KD, 1)

    def layer(self, l):
        L = self.nlayers
        self.pf_limit = LAST_GATE
        self.layer_mods(l)
        self.ffn(l, 0)
        if self.stop == "ffn1":
            return False
        self.inproj(l)
        if self.nrounds is None:
            self.pool_rounds(l * NR_L + R1, (l + 1) * NR_L)
            if l + 1 < L:
                self.pool_rounds((l + 1) * NR_L, (l + 1) * NR_L + R1)
        if self.stop == "inproj":
            return False
        self.branches(l)
        if self.stop == "branches":
            return False
        self.merge(l)
        if self.stop == "mixer":
            return False
        self.ffn(l, 1)
        return True

    def finish(self):
        sp = self.sp
        evs = []
        for piece in range(4):
            evs.append(self.mdma(sp, self.outT[:, 8 * piece:8 * (piece + 1), :], self.hT[:, 8 * piece:8 * (piece + 1), 0:TL],
                                 waits=self.h_ev))
        for name in self.dbg:
            src = getattr(self, name)
            dst = self.dram("dbg_" + name, list(src.shape), src.dtype, "ExternalOutput")
            w = self.h_ev + getattr(self, "ys_ev", []) + getattr(self, "x_ev", []) + getattr(self, "pgate_ev", []) + getattr(self, "pcx_ev", []) \
                + getattr(self, "pu_ev", []) + getattr(self, "pvn_ev", []) + getattr(self, "qk_ev", [])
            evs.append(self.mdma(sp, dst.ap().opt(), src.ap().opt(), waits=w))
        self.wait(sp, evs)
        if self.pool_cc_pending is not None:
            self.wait(self.pool, self.pool_cc_pending)


def kernel(**inputs):
    nl = DEPTH
    maps = host_inputs(inputs, nl)
    nc = Builder(nl, "full").build()
    res = run_bass_kernel_spmd(nc, maps, core_ids=list(range(NCORES)))
    out = np.empty((1, SEQ, D), np.float32)
    for c in range(NCORES):
        oT = np.asarray(res.results[c]["outT"])
        out[0, TL * c:TL * (c + 1), :] = oT.transpose(2, 1, 0).reshape(TL, D)
    return out
```

```python
import contextlib
import numpy as np
import ml_dtypes
import concourse.bass as bass
import concourse.mybir as mybir
from concourse.bass_utils import run_bass_kernel_spmd

F32 = mybir.dt.float32
BF16 = mybir.dt.bfloat16
AF = mybir.ActivationFunctionType
ALU = mybir.AluOpType
AX = mybir.AxisListType
NPBF = ml_dtypes.bfloat16

NCORES = 8
DEPTH = 4
D = 4096
KD = 32
TL = 1024
TCX = 32
T = TL + TCX
NTS = [(0, 352), (352, 352), (704, 352)]
DFF = 3584
KF = 28
SEQ = 8192
CTX = 256
HD = 128
EPS = 1e-6
TILE_W = 128
BLK = 32
NR_L = 54
IN_COLS = 24064
GATE0 = 7680
XU = 6
XK = 0
XV = 576
XVC = 1088
XVN = 1344
XZE = 2368
XZ = 4096


def pos_of_rank(r):
    return ((r >> 1) & 1) * 4 + (r >> 2) * 2 + (r & 1)


RANK_AT_POS = [0] * 8
for _r in range(8):
    RANK_AT_POS[pos_of_rank(_r)] = _r


def layer_stream():
    S = []

    def add(name, mat, sub, col0, kc, idx=0, r=0):
        S.append(dict(name=name, mat=mat, sub=sub, col0=col0, kc=kc, nt=(32 if kc > 8 else 8), idx=idx, r=r))

    def ffn(f):
        for s in range(28):
            add("ffn_a", "ffn_wi", f, 128 * s, 32, s)
            add("ffn_b", "ffn_wi", f, DFF + 128 * s, 32, s)
        for j in range(32):
            add("ffn_o", "ffn_wo", f, 128 * j, 28, j)

    ffn(0)
    for i in range(2):
        add("k", "w_in", None, 1024 + 128 * i, 32, i)
    for i in range(2):
        add("v", "w_in", None, 1280 + 128 * i, 32, i)
    for i in range(8):
        add("q", "w_in", None, 128 * i, 32, i)
    for i in range(8):
        add("f", "w_in", None, 6656 + 128 * i, 32, i)
    for i in range(8):
        add("sv", "w_in", None, 2560 + 128 * i, 32, i)
    for i in range(8):
        add("su", "w_in", None, 1536 + 128 * i, 32, i)
    for i in range(8):
        add("cx", "w_in", None, 3584 + 128 * i, 32, i)
        add("cc", "w_in", None, 5632 + 128 * i, 32, i)
        add("cb", "w_in", None, 4608 + 128 * i, 32, i)
    for j in range(32):
        for r in range(4):
            add("gate", "w_in", None, GATE0 + r * D + 128 * j, 32, j, r)
    for j in range(32):
        for r in range(4):
            add("br", "w_branch", r, 128 * j, 8, j, r)
    for j in range(32):
        add("wo", "w_out", None, 128 * j, 32, j)
    ffn(1)
    t = 0
    for i, s in enumerate(S):
        s["t0"] = t
        s["i"] = i
        t += s["nt"]
    assert t <= NR_L * 8 * BLK, t
    return S


STREAM = layer_stream()
LAST_GATE = max(s["i"] for s in STREAM if s["name"] == "gate")
WSL_N = 6
PF_D = WSL_N - 2
R1 = -(-max(s["t0"] + s["nt"] for s in STREAM if s["name"] == "gate") // (8 * BLK))


def pack_weights(inputs, nlayers):
    big = np.zeros((NCORES, nlayers * NR_L, 128, BLK, TILE_W), np.float32)
    for l in range(nlayers):
        for s in STREAM:
            W = inputs[s["mat"]][l]
            if s["sub"] is not None:
                W = W[s["sub"]]
            kc = s["kc"]
            Wc = np.asarray(W[:, s["col0"]:s["col0"] + TILE_W]).reshape(kc, 128, TILE_W)
            t0 = s["t0"]
            i = 0
            while i < kc:
                b = (t0 + i) // BLK
                j = (t0 + i) % BLK
                n = min(kc - i, BLK - j)
                big[b % 8, l * NR_L + b // 8, :, j:j + n, :] = Wc[i:i + n].transpose(1, 0, 2)
                i += n
    return big.reshape(NCORES, nlayers * NR_L, 128, BLK * TILE_W)


def fm(v):
    v = np.asarray(v, np.float32)
    return v.reshape(-1, 128).T.copy()


def host_consts(core):
    c = core
    out = {}
    pos = np.arange(TL) + TL * c
    row = (pos // 64).astype(np.float32)
    col = (pos % 64).astype(np.float32)
    ax = HD // 2
    inv = (1.0 / (np.float32(10000.0) ** (np.arange(0, ax, 2, dtype=np.float32) / np.float32(ax)))).astype(np.float32)
    ang_r = row[:, None] * inv[None, :]
    ang_c = col[:, None] * inv[None, :]
    emb = np.concatenate([ang_r, ang_r, ang_c, ang_c], axis=-1).astype(np.float32)
    cs = np.zeros((128, 2, T), np.float32)
    cs[:, 0, :TL] = np.cos(emb).T
    cs[:, 1, :TL] = np.sin(emb).T
    cs[:, 0, TL:] = 1.0
    out["cs"] = cs
    R = np.zeros((128, 128), np.float32)
    for i in range(128):
        seg = (i // 32) % 2
        if seg == 0:
            R[i + 32, i] = -1.0
        else:
            R[i - 32, i] = 1.0
    out["rot"] = R.astype(NPBF)
    kp = np.arange(128)[:, None]
    qp = np.arange(128)[None, :]
    mP = (kp >= qp).astype(np.float32)
    mN = (kp <= qp).astype(np.float32)
    m = np.zeros((128, 4, 4, 128), np.float32)
    m[:, 0] = mP[:, None, :]
    m[:, 1] = mN[:, None, :]
    m[:, 2] = mP[:, None, :] if c > 0 else 0.0
    m[:, 3] = mN[:, None, :] if c < NCORES - 1 else 0.0
    out["masks"] = m.reshape(128, 4, 512).astype(NPBF)
    oh = np.zeros((128, 2, 8), np.float32)
    if c > 0:
        oh[:, 0, pos_of_rank(c - 1)] = 1.0
    if c < NCORES - 1:
        oh[:, 1, pos_of_rank(c + 1)] = 1.0
    out["oh"] = oh
    ch = np.arange(256)
    ang = 2.0 * np.pi * ((ch[:, None] * ch[None, :]) % 256) / 256.0
    cs256 = np.concatenate([np.cos(ang), np.sin(ang)], axis=1) / 16.0
    out["cs256"] = cs256.reshape(2, 128, 512).transpose(1, 0, 2).astype(NPBF)
    tab = np.arange(SEQ)
    ctab = (np.cos(2.0 * np.pi * tab / SEQ) / np.sqrt(SEQ)).astype(np.float32)
    stab = (-np.sin(2.0 * np.pi * tab / SEQ) / np.sqrt(SEQ)).astype(np.float32)
    s_idx = (np.arange(TL) + TL * c).astype(np.int64)
    t_idx = np.arange(SEQ, dtype=np.int64)
    prod = (t_idx[:, None] * s_idx[None, :]) % SEQ
    cts = np.empty((SEQ, 2, TL), NPBF)
    cts[:, 0, :] = ctab[prod].astype(NPBF)
    cts[:, 1, :] = stab[prod].astype(NPBF)
    out["cts"] = cts.reshape(64, 128, 2, TL).transpose(1, 0, 2, 3).copy()
    tc_ = np.arange(CTX, dtype=np.int64)
    sc_ = (np.arange(TCX) + TCX * c).astype(np.int64)
    pc = (tc_[:, None] * sc_[None, :]) % CTX
    c2 = np.empty((CTX, 2, TCX), np.float32)
    c2[:, 0, :] = np.cos(2.0 * np.pi * pc / CTX) / 16.0
    c2[:, 1, :] = -np.sin(2.0 * np.pi * pc / CTX) / 16.0
    out["ct256"] = c2.reshape(2, 128, 2, TCX).transpose(1, 0, 2, 3).astype(NPBF).copy()
    return out


def host_inputs(inputs, nlayers):
    x = np.asarray(inputs["x"], np.float32)[0]
    ctx = np.asarray(inputs["ctx"], np.float32)[0]
    wts = pack_weights(inputs, nlayers)
    cT = np.stack([fm(inputs["c"][0]), fm(inputs["c_ctx"])], axis=-1)
    adown = np.asarray(inputs["ada_down"], np.float32)[:nlayers].reshape(nlayers, KD, 128, 256).transpose(2, 0, 1, 3).copy()
    ng = np.asarray(inputs["norm_g"], np.float32)[:nlayers].reshape(nlayers, 3, KD, 128).transpose(3, 0, 1, 2).copy()
    qk = np.stack([np.asarray(inputs["q_norm"], np.float32)[:nlayers].T,
                   np.asarray(inputs["k_norm"], np.float32)[:nlayers].T], axis=-1).copy()
    sink = np.asarray(inputs["sink"], np.float32)[:nlayers].reshape(1, nlayers * 8)
    lng = np.asarray(inputs["sgu_ln_g"], np.float32)[:nlayers].reshape(1, nlayers * 1024)
    lnb = np.asarray(inputs["sgu_ln_b"], np.float32)[:nlayers].reshape(1, nlayers * 1024)
    sw = np.asarray(inputs["sgu_w"], np.float32)[:nlayers]
    wsT = sw.transpose(3, 0, 1, 2).copy()
    sb_ = np.asarray(inputs["sgu_b"], np.float32)[:nlayers]
    cw = np.asarray(inputs["conv_w"], np.float32)[:nlayers].reshape(nlayers, 3, 8, 128).transpose(3, 0, 2, 1).copy()
    aup_full = np.asarray(inputs["ada_up"], np.float32)[:nlayers]
    ab_full = np.asarray(inputs["ada_b"], np.float32)[:nlayers]
    maps = []
    for c in range(NCORES):
        tok = np.concatenate([x[TL * c:TL * (c + 1)], ctx[TCX * c:TCX * (c + 1)]], axis=0)
        xT = tok.T.reshape(KD, 128, T).transpose(1, 0, 2).copy()
        aup = aup_full[:, :, 4608 * c:4608 * (c + 1)].reshape(nlayers, 2, 128, 4608).transpose(2, 0, 1, 3).copy()
        ab = ab_full[:, 4608 * c:4608 * (c + 1)].reshape(nlayers, 36, 128).transpose(2, 0, 1).copy()
        wsc = np.zeros((128, nlayers, 2, 8, TCX), np.float32)
        p0 = TCX * (c % 4)
        wsc[:, :, c // 4, :, :] = sw[:, :, p0:p0 + TCX, :].transpose(3, 0, 1, 2)
        bsc = sb_[:, :, p0 + 128 * 0:p0 + TCX].reshape(1, nlayers * 8 * TCX).copy()
        m = dict(xT=xT, wts=wts[c], cT=cT, adown=adown, aup=aup, ab=ab, ng=ng, qk=qk, sink=sink,
                 lng=lng, lnb=lnb, wsT=wsT, wsc=wsc, bsT=sb_.reshape(1, nlayers * 1024).copy(), bsc=bsc, cw=cw)
        m.update(host_consts(c))
        maps.append(m)
    return maps


class Sem:
    def __init__(self, h):
        self.h = h
        self.v = 0

    def inc(self, ins, n=1):
        ins.then_inc(self.h, n)
        self.v += n
        return (self, self.v)


class Ring:
    def __init__(self, b, name, n, shape, dt):
        self.n = n
        self.t = [b.sb(f"{name}{i}", shape, dt) for i in range(n)]
        self.dsem = [b.sem(f"{name}_d{i}") for i in range(n)]
        self.free = [[] for _ in range(n)]
        self.i = 0

    def take(self):
        s = self.i % self.n
        self.i += 1
        return s


class Builder:
    def __init__(self, nlayers=DEPTH, stop="full", dbg=(), nrounds=None):
        self.nlayers = nlayers
        self.stop = stop
        self.dbg = dbg
        self.nrounds = nrounds
        self.nc = bass.Bass("TRN2", target_bir_lowering=False)
        self.es = contextlib.ExitStack()
        self.waited = {}

    def sb(self, name, shape, dt):
        return self.es.enter_context(self.nc.sbuf_tensor(name, list(shape), dt))

    def sem(self, name):
        return Sem(self.es.enter_context(self.nc.semaphore(name)))

    def dram(self, name, shape, dt, kind=None):
        if kind is None:
            return self.nc.dram_tensor(name, list(shape), dt)
        return self.nc.dram_tensor(name, list(shape), dt, kind=kind)

    def wait(self, eng, ev):
        if ev is None:
            return
        if isinstance(ev, list):
            for e in ev:
                self.wait(eng, e)
            return
        sem, v = ev
        key = (id(eng), id(sem))
        if self.waited.get(key, 0) >= v:
            return
        self.waited[key] = v
        eng.wait_ge(sem.h, v)

    def ev(self, eng, ins):
        return self.esem[id(eng)].inc(ins, 1)

    def op(self, eng, waits, ins_fn):
        self.wait(eng, waits)
        return self.ev(eng, ins_fn())

    def dma(self, q, out, in_, sem, waits=None):
        self.wait(q, waits)
        ins = q.dma_start(out=out, in_=in_)
        return sem.inc(ins, 16)

    def mdma(self, q, out, in_, waits=None):
        k = self.misc_i % len(self.misc_sems)
        self.misc_i += 1
        sem = self.misc_sems[k]
        self.wait(q, self.misc_last[k])
        ev = self.dma(q, out, in_, sem, waits)
        self.misc_last[k] = ev
        return ev

    def ps_take(self):
        bnk = self.ps_i % 8
        self.ps_i += 1
        n_try = 0
        while self.ps_held[bnk]:
            bnk = self.ps_i % 8
            self.ps_i += 1
            n_try += 1
            assert n_try < 8, "all psum banks held"
        self.ps_held[bnk] = True
        self.wait(self.pe, self.ps_free[bnk])
        self.ps_free[bnk] = []
        return bnk

    def ps_release(self, bnk, evs):
        self.ps_held[bnk] = False
        self.ps_free[bnk] = list(evs) if isinstance(evs, list) else [evs]

    def pg(self, i, n=1, dt=BF16):
        if n == 1:
            v = self.bufA[:, i, :]
        else:
            v = self.bufA[:, i:i + n, :].rearrange("p a b -> p (a b)")
        return v if dt == BF16 else v.bitcast(dt)

    def build(self):
        with self.es:
            self._build()
        return self.nc

    def _build(self):
        nc = self.nc
        L = self.nlayers
        self.pe, self.act, self.dve, self.pool, self.sp = nc.tensor, nc.scalar, nc.vector, nc.gpsimd, nc.sync
        self.esem = {id(self.pe): self.sem("t_pe"), id(self.act): self.sem("t_act"), id(self.dve): self.sem("t_dve"),
                     id(self.pool): self.sem("t_pool")}
        NRT = L * NR_L
        di = lambda n, s, d=F32: self.dram(n, s, d, "ExternalInput")
        self.xT = di("xT", [128, KD, T])
        self.wts = di("wts", [self.nrounds or NRT, 128, BLK * TILE_W])
        self.cT = di("cT", [128, KD, 2])
        self.adown = di("adown", [128, L, KD, 256])
        self.aup = di("aup", [128, L, 2, 4608])
        self.ab = di("ab", [128, L, 36])
        self.ng = di("ng", [128, L, 3, KD])
        self.qk = di("qk", [128, L, 2])
        self.sink = di("sink", [1, L * 8])
        self.lng = di("lng", [1, L * 1024])
        self.lnb = di("lnb", [1, L * 1024])
        self.wsT = di("wsT", [128, L, 8, 128])
        self.wsc = di("wsc", [128, L, 2, 8, TCX])
        self.bsT = di("bsT", [1, L * 1024])
        self.bsc = di("bsc", [1, L * 8 * TCX])
        self.cw = di("cw", [128, L, 8, 3])
        self.cs = di("cs", [128, 2, T])
        self.rot = di("rot", [128, 128], BF16)
        self.masks = di("masks", [128, 4, 512], BF16)
        self.oh = di("oh", [128, 2, 8])
        self.cs256 = di("cs256", [128, 2, 512], BF16)
        self.cts = di("cts", [128, 64, 2, TL], BF16)
        self.ct256 = di("ct256", [128, 2, 2, TCX], BF16)
        self.outT = self.dram("outT", [128, KD, TL], F32, "ExternalOutput")
        self.wb = self.dram("wb", [NRT, 128, BLK * TILE_W], BF16)
        self.g1 = {}
        self.wg = [self.dram(f"wg{i}", [32, 8, 128, BLK, TILE_W], BF16) for i in range((NRT + 31) // 32)]
        self.hT = self.dram("hT", [128, KD, T], F32)
        self.gT = self.dram("gT", [128, KF, T], BF16)
        self.mT = self.dram("mT", [128, KD, T], BF16)
        self.Pys = self.dram("Pys", [128, KD, T], BF16)
        self.Pq = self.dram("Pq", [128, 8, T], BF16)
        self.Pk = self.dram("Pk", [128, 2, T], BF16)
        self.Pv = self.dram("Pv", [T, 256], BF16)
        self.Pu = self.dram("Pu", [128, 8, T], BF16)
        self.Pvn = self.dram("Pvn", [T, 1024], BF16)
        self.Pcx = self.dram("Pcx", [128, 3, 8, T], F32)
        self.Pgate = self.dram("Pgate", [128, 4, KD, T], BF16)
        self.xin = [self.dram(f"xin{u}", [128, 4096], BF16) for u in range(XU)]
        self.xall = [self.dram(f"xall{u}", [8, 128, 4096], BF16) for u in range(XU)]
        self.xinz = self.dram("xinz", [128, 32], F32)
        self.zall = self.dram("zall", [8, 128, 32], F32)
        self.min_ = self.dram("min_", [128, L * 72], F32)
        self.mall = self.dram("mall", [8, 128, L * 72], F32)
        self.bufA = self.sb("bufA", [128, KD, T], BF16)
        self.wsl = Ring(self, "wsl", WSL_N, [128, 32, TILE_W], BF16)
        self.hch = Ring(self, "hch", 4, [128, T], F32)
        self.hout = Ring(self, "hout", 2, [128, T], F32)
        self.tmpf = Ring(self, "tmpf", 2, [128, T], F32)
        self.sil = Ring(self, "sil", 4, [128, 352], F32)
        self.gst = Ring(self, "gst", 4, [128, T], BF16)
        self.sqb = Ring(self, "sqb", 3, [128, T], BF16)
        self.stf = Ring(self, "stf", 2, [128, 1024], F32)
        self.stb = Ring(self, "stb", 2, [128, 1024], BF16)
        self.rstd = self.sb("rstd", [128, T], F32)
        self.mods = self.sb("mods", [128, L, 288, 2], F32)
        self.ngs = self.sb("ngs", [128, L, 3, KD], F32)
        self.modA = self.sb("modA", [128, 3, KD, 2], F32)
        self.modG = self.sb("modG", [128, 3, KD, 2], F32)
        self.ones_b = self.sb("ones_b", [128, 128], BF16)
        self.eps_t = self.sb("eps_t", [128, 1], F32)
        self.cs256_t = self.sb("cs256_t", [128, 2, 512], BF16)
        self.ct256_t = self.sb("ct256_t", [128, 2, 2, TCX], BF16)
        self.rot_b = self.sb("rot_b", [128, 128], BF16)
        self.qkg = self.sb("qkg", [128, L, 2], F32)
        self.esk = self.sb("esk", [128, L * 8], F32)
        self.cw_t = self.sb("cw_t", [128, L, 8, 3], F32)
        self.oh_t = self.sb("oh_t", [128, 2, 8], F32)
        self.xe = self.sb("xe", [128, 8, 4], F32)
        self.ce = self.sb("ce", [128, 8, 4], F32)
        self.zeo = self.sb("zeo", [128, 32], F32)
        self.selz = self.sb("selz", [128, 2, 32], F32)
        self.stt = self.sb("stt", [128, 16], F32)
        self.ps = self.es.enter_context(nc.psum_tensor("ps", [128, 8, 512], F32))
        self.ps_i = 0
        self.ps_free = [[] for _ in range(8)]
        self.ps_held = [False] * 8
        self.s_cc = self.sem("cc")
        self.s_cc1 = self.sem("cc1")
        self.misc_sems = [self.sem(f"misc{i}") for i in range(12)]
        self.misc_last = [None] * 12
        self.misc_i = 0
        self.round_ev = {}
        self.pf_layer, self.pf_ptr, self.pf_issued, self.pf_limit = -1, 0, {}, LAST_GATE
        self.pool_cc_pending = None
        self.g_n = 0
        self.g_pend = None
        self.g1_busy = [None, None, None]
        self.l1_last = [None, None]
        self.l2_last = [None, None]
        self.s_l1 = [self.sem('cl1a'), self.sem('cl1b')]
        self.s_l2 = [self.sem('cl2a'), self.sem('cl2b')]
        self.bufA_free = []
        self.rstd_free = []
        self.modAG_free = []
        dve, act, sp = self.dve, self.act, self.sp

        e1 = self.ev(dve, dve.memset(self.ones_b[:], 1.0))
        e2 = self.ev(dve, dve.memset(self.eps_t[:], EPS))
        cev = [e1, e2]
        for dst, src in ((self.cs256_t, self.cs256), (self.ct256_t, self.ct256), (self.rot_b, self.rot), (self.qkg, self.qk),
                         (self.cw_t, self.cw), (self.oh_t, self.oh)):
            cev.append(self.mdma(sp, dst[:], src[:]))
        esk_l = self.mdma(sp, self.esk[:], self.sink[0:1, :].partition_broadcast(128))
        cev.append(self.op(act, [esk_l], lambda: act.activation(out=self.esk[:], in_=self.esk[:], func=AF.Exp)))
        self.const_ev = cev
        for e in (self.pe, self.act, self.dve):
            self.wait(e, cev)

        self.h_ev = [self.mdma(sp, self.hT[:], self.xT[:])]
        self.adaln()
        self.pool_rounds(0, min(R1, self.nrounds or R1))
        for l in range(L):
            done = self.layer(l)
            if not done:
                break
        self.finish()

    def pool_rounds(self, t0, t1):
        pool = self.pool
        for t in range(t0, t1):
            oevs = []
            for q in range(4):
                sf = self.stf.take()
                ein = self.dma(pool, self.stf.t[sf][:], self.wts[t, :, 1024 * q:1024 * (q + 1)], self.stf.dsem[sf],
                               waits=self.stf.free[sf])
                sb_ = self.stb.take()
                self.wait(pool, [ein] + self.stb.free[sb_])
                ec = self.ev(pool, pool.tensor_copy(out=self.stb.t[sb_][:], in_=self.stf.t[sf][:]))
                self.stf.free[sf] = [ec]
                eo = self.dma(pool, self.wb[t, :, 1024 * q:1024 * (q + 1)], self.stb.t[sb_][:], self.stb.dsem[sb_], waits=[ec])
                self.stb.free[sb_] = [eo]
                oevs.append(eo)
            self.wait(pool, oevs)
            prev = self.gather(self.wb[t], self.wg[t // 32][t % 32].rearrange("a p b c -> a p (b c)"), BF16)
            if t > t0:
                self.round_ev[t - 1] = prev
        if t1 > t0:
            self.round_ev[t1 - 1] = self.gather_flush()

    def gather(self, src, dst, dt):
        pool = self.pool
        X = src.shape[-1]
        key = (X, str(dt))
        if key not in self.g1:
            self.g1[key] = [self.dram(f"g1_{X}_{len(self.g1)}_{i}", [4 * 128, X], dt) for i in range(3)]
        n = self.g_n
        self.g_n += 1
        g1 = self.g1[key][n % 3]
        self.wait(pool, self.g1_busy[n % 3])
        self.wait(pool, self.l1_last[n % 2])
        i1 = pool.collective_compute("AllGather", ALU.bypass, replica_groups=[[0, 1, 2, 3], [4, 5, 6, 7]],
                                     ins=[src], outs=[g1[:, :]])
        e1 = self.s_l1[n % 2].inc(i1, 1)
        self.l1_last[n % 2] = e1
        prev_done = self._gather_level2()
        self.g_pend = (n, g1, dst, e1)
        return prev_done

    def _gather_level2(self):
        pool = self.pool
        if self.g_pend is None:
            return None
        n, g1, dst, e1 = self.g_pend
        self.g_pend = None
        self.wait(pool, e1)
        self.wait(pool, self.l2_last[n % 2])
        for h in range(2):
            i2 = pool.collective_compute("AllGather", ALU.bypass, replica_groups=[[0, 4], [1, 5], [2, 6], [3, 7]],
                                         ins=[g1[256 * h:256 * (h + 1), :]],
                                         outs=[dst[4 * h:4 * (h + 1)].rearrange("a p x -> (a p) x")])
            e2 = self.s_l2[n % 2].inc(i2, 1)
        self.g1_busy[n % 3] = e2
        self.l2_last[n % 2] = e2
        self.pool_cc_pending = e2
        return e2

    def gather_flush(self):
        return self._gather_level2()

    def adaln(self):
        L = self.nlayers
        act, dve, pe, sp = self.act, self.dve, self.pe, self.sp
        cs = self.sb("ada_c", [128, KD, 2], F32)
        tT = self.sb("ada_t", [128, 2, 2], F32)
        mloc = self.sb("ada_m", [128, L, 36, 2], F32)
        abs_ = self.sb("ada_ab", [128, L, 36], F32)
        e = self.mdma(sp, cs[:], self.cT[:])
        e_ab = self.mdma(sp, abs_[:], self.ab[:])
        e_ng = self.mdma(sp, self.ngs[:], self.ng[:])
        ec = self.op(act, [e], lambda: act.activation(out=cs[:], in_=cs[:], func=AF.Silu))
        for l in range(L):
            for half in range(2):
                bnk = None
                for piece in range(8):
                    s = self.hch.take()
                    wv = self.hch.t[s][:, 0:512].rearrange("p (k m) -> p k m", k=4)
                    el = self.dma(sp, wv, self.adown[:, l, 4 * piece:4 * piece + 4, 128 * half:128 * half + 128],
                                  self.hch.dsem[s], waits=self.hch.free[s])
                    if piece == 0:
                        bnk = self.ps_take()
                    self.wait(pe, [el, ec])
                    for k in range(4):
                        kk = 4 * piece + k
                        mm = pe.matmul(self.ps[:, bnk, 0:2], lhsT=wv[:, k, :], rhs=cs[:, kk, :], start=(kk == 0), stop=(kk == KD - 1))
                    em = self.ev(pe, mm)
                    self.hch.free[s] = [em]
                et = self.op(dve, [em], lambda: dve.tensor_copy(out=tT[:, half, :], in_=self.ps[:, bnk, 0:2]))
                self.ps_release(bnk, et)
            for grp in range(9):
                s = self.hch.take()
                wv = self.hch.t[s][:, 0:1024].rearrange("p (k m) -> p k m", k=2)
                el = self.dma(sp, wv, self.aup[:, l, :, 512 * grp:512 * (grp + 1)], self.hch.dsem[s], waits=self.hch.free[s])
                bnk = self.ps_take()
                self.wait(pe, [el, et])
                for j in range(4):
                    for k in range(2):
                        mm = pe.matmul(self.ps[:, bnk, 2 * j:2 * j + 2], lhsT=wv[:, k, 128 * j:128 * (j + 1)], rhs=tT[:, k, :],
                                       start=(k == 0), stop=(k == 1))
                em = self.ev(pe, mm)
                self.hch.free[s] = [em]
                self.wait(dve, [em, e_ab])
                for j in range(4):
                    ch = 4 * grp + j
                    eo = self.ev(dve, dve.tensor_scalar(out=mloc[:, l, ch, :], in0=self.ps[:, bnk, 2 * j:2 * j + 2],
                                                        scalar1=abs_[:, l, ch:ch + 1], scalar2=None, op0=ALU.add))
                self.ps_release(bnk, eo)
        es_ = self.mdma(sp, self.min_[:, :], mloc[:].rearrange("p l c t -> p (l c t)"), waits=[eo])
        self.wait(self.pool, es_)
        self.gather(self.min_[:, :], self.mall, F32)
        eg = self.gather_flush()
        evs = []
        for pos in range(8):
            r = RANK_AT_POS[pos]
            evs.append(self.mdma(sp, self.mods[:, :, 36 * r:36 * (r + 1), :],
                                 self.mall[pos].rearrange("p (l c t) -> p l c t", l=L, c=36), waits=[eg]))
        self.mods_ev = evs + [e_ng]

    def layer_mods(self, l):
        dve = self.dve
        self.wait(dve, self.mods_ev + self.modAG_free)
        for i in range(3):
            for t in range(2):
                dve.scalar_tensor_tensor(out=self.modA[:, i, :, t], in0=self.mods[:, l, (3 * i + 1) * 32:(3 * i + 2) * 32, t],
                                         scalar=1.0, in1=self.ngs[:, l, i, :], op0=ALU.add, op1=ALU.mult)
                e = self.ev(dve, dve.tensor_scalar(out=self.modG[:, i, :, t], in0=self.mods[:, l, (3 * i + 2) * 32:(3 * i + 3) * 32, t],
                                                   scalar1=(1.0 if i == 1 else 0.5), scalar2=None, op0=ALU.mult))
        self.modAG_ev = e

    def _issue_slab(self, l, s):
        sp = self.sp
        slot = self.wsl.take()
        t0, kc = s["t0"], s["kc"]
        b = t0 // BLK
        j = t0 % BLK
        rnd = l * NR_L + b // 8
        pos = pos_of_rank(b % 8)
        ev = self.dma(sp, self.wsl.t[slot][:, 0:kc, :], self.wg[rnd // 32][rnd % 32, pos, :, j:j + kc, :], self.wsl.dsem[slot],
                      waits=self.wsl.free[slot] + [self.round_ev[rnd]])
        return slot, ev

    def load_slab(self, l, s):
        i = s["i"]
        if self.pf_layer != l:
            self.pf_layer, self.pf_ptr, self.pf_issued = l, 0, {}
        assert i >= self.pf_ptr - len(self.pf_issued), (i, self.pf_ptr)
        target = max(i, min(i + PF_D, self.pf_limit))
        while self.pf_ptr <= target:
            sj = STREAM[self.pf_ptr]
            self.pf_issued[self.pf_ptr] = self._issue_slab(l, sj)
            self.pf_ptr += 1
        return self.pf_issued.pop(i)

    def sl(self, name, sub=None):
        return [s for s in STREAM if s["name"] == name and (sub is None or s["sub"] == sub)]

    def mm_fm(self, slot, lev, kc, c0, w, in_ev, rhs0=0):
        pe = self.pe
        bnk = self.ps_take()
        self.wait(pe, [lev] + in_ev)
        for k in range(kc):
            mm = pe.matmul(self.ps[:, bnk, 0:w], lhsT=self.wsl.t[slot][:, k, :], rhs=self.bufA[:, rhs0 + k, c0:c0 + w],
                           start=(k == 0), stop=(k == kc - 1))
        return bnk, self.ev(pe, mm)

    def mm_tm(self, slot, lev, tb, in_ev):
        pe = self.pe
        M = 128 if tb < 8 else TCX
        bnk = self.ps_take()
        self.wait(pe, [lev] + in_ev)
        for k in range(KD):
            mm = pe.matmul(self.ps[0:M, bnk, 0:128], lhsT=self.bufA[:, k, 128 * tb:128 * tb + M], rhs=self.wsl.t[slot][:, k, :],
                           start=(k == 0), stop=(k == KD - 1))
        return bnk, self.ev(pe, mm), M

    def norm(self, l, i):
        act, dve, pe, sp = self.act, self.dve, self.pe, self.sp
        if getattr(self, "ss_banks", None) is not None:
            banks, em = self.ss_banks, self.ss_ev
            self.ss_banks = None
        else:
            banks = [self.ps_take() for _ in range(3)]
            for kc in range(KD):
                s = self.hch.take()
                el = self.dma(sp, self.hch.t[s][:], self.hT[:, kc, :], self.hch.dsem[s], waits=self.hch.free[s] + self.h_ev)
                q = self.sqb.take()
                ea = self.op(act, [el] + self.sqb.free[q], lambda: act.activation(out=self.sqb.t[q][:], in_=self.hch.t[s][:], func=AF.Square))
                self.hch.free[s] = [ea]
                self.wait(pe, [ea])
                for n, (c0, w) in enumerate(NTS):
                    mm = pe.matmul(self.ps[:, banks[n], 0:w], lhsT=self.ones_b[:], rhs=self.sqb.t[q][:, c0:c0 + w],
                                   start=(kc == 0), stop=(kc == KD - 1))
                em = self.ev(pe, mm)
                self.sqb.free[q] = [em]
        self.wait(act, [em] + self.rstd_free)
        for n, (c0, w) in enumerate(NTS):
            ea = self.ev(act, act.activation(out=self.rstd[:, c0:c0 + w], in_=self.ps[:, banks[n], 0:w], func=AF.Sqrt,
                                             scale=1.0 / D, bias=self.eps_t[:, 0:1]))
        for bnk in banks:
            self.ps_release(bnk, ea)
        er = self.op(dve, [ea], lambda: dve.reciprocal(out=self.rstd[:], in_=self.rstd[:]))
        self.wait(act, [self.modAG_ev] + self.bufA_free)
        for kc in range(KD):
            s = self.hch.take()
            el = self.dma(sp, self.hch.t[s][:], self.hT[:, kc, :], self.hch.dsem[s], waits=self.hch.free[s] + self.h_ev)
            tq = self.tmpf.take()
            ed = self.op(dve, [el, er] + self.tmpf.free[tq],
                         lambda: dve.tensor_tensor(out=self.tmpf.t[tq][:], in0=self.hch.t[s][:], in1=self.rstd[:], op=ALU.mult))
            self.hch.free[s] = [ed]
            self.wait(act, ed)
            sh = 3 * i * 32 + kc
            act.activation(out=self.bufA[:, kc, 0:TL], in_=self.tmpf.t[tq][:, 0:TL], func=AF.Identity,
                           scale=self.modA[:, i, kc:kc + 1, 0], bias=self.mods[:, l, sh:sh + 1, 0])
            ea = self.ev(act, act.activation(out=self.bufA[:, kc, TL:T], in_=self.tmpf.t[tq][:, TL:T], func=AF.Identity,
                                             scale=self.modA[:, i, kc:kc + 1, 1], bias=self.mods[:, l, sh:sh + 1, 1]))
            self.tmpf.free[tq] = [ea]
        self.rstd_free = [ed]
        self.bufA_ev = [ea]
        self.modAG_free = [ea]

    def ffn(self, l, f):
        act, dve, sp = self.act, self.dve, self.sp
        i = 0 if f == 0 else 2
        self.norm(l, i)
        A, Bs, O = self.sl("ffn_a", f), self.sl("ffn_b", f), self.sl("ffn_o", f)
        gstore = []
        for pair in range(28):
            sa, la = self.load_slab(l, A[pair])
            sb_, lb = self.load_slab(l, Bs[pair])
            g = self.gst.take()
            for n, (c0, w) in enumerate(NTS):
                ba, ema = self.mm_fm(sa, la, KD, c0, w, self.bufA_ev)
                bb, emb = self.mm_fm(sb_, lb, KD, c0, w, self.bufA_ev)
                q = self.sil.take()
                ea = self.op(act, [ema] + self.sil.free[q], lambda: act.activation(out=self.sil.t[q][:, 0:w], in_=self.ps[:, ba, 0:w], func=AF.Silu))
                self.ps_release(ba, ea)
                ed = self.op(dve, [emb, ea] + (self.gst.free[g] if n == 0 else []),
                             lambda: dve.tensor_tensor(out=self.gst.t[g][:, c0:c0 + w], in0=self.sil.t[q][:, 0:w], in1=self.ps[:, bb, 0:w], op=ALU.mult))
                self.ps_release(bb, ed)
                self.sil.free[q] = [ed]
            self.wsl.free[sa] = [ema]
            self.wsl.free[sb_] = [emb]
            es_ = self.dma(sp, self.gT[:, pair, :], self.gst.t[g][:], self.gst.dsem[g], waits=[ed])
            self.gst.free[g] = [es_]
            gstore.append(es_)
        evs = []
        for piece in range(4):
            evs.append(self.mdma(sp, self.bufA[:, 7 * piece:7 * (piece + 1), :], self.gT[:, 7 * piece:7 * (piece + 1), :],
                                 waits=gstore + [emb]))
        self.bufA_ev = evs
        self.down_proj(l, O, KF, i, fuse_ss=not (f == 1 and l == self.nlayers - 1))

    def down_proj(self, l, O, kc, i, fuse_ss=True):
        dve, sp, act, pe = self.dve, self.sp, self.act, self.pe
        stores = []
        hl_q = {}
        ss_banks = [self.ps_take() for _ in range(3)] if fuse_ss else None
        pend = None

        def ss_mm(p, first, last):
            q, ea = p
            self.wait(pe, [ea])
            for n_, (c0_, w_) in enumerate(NTS):
                mm_ = pe.matmul(self.ps[:, ss_banks[n_], 0:w_], lhsT=self.ones_b[:], rhs=self.sqb.t[q][:, c0_:c0_ + w_], start=first, stop=last)
            em_ = self.ev(pe, mm_)
            self.sqb.free[q] = [em_]
            return em_

        def issue_h(c):
            hs = self.hch.take()
            hl_q[c] = (hs, self.dma(sp, self.hch.t[hs][:], self.hT[:, c, :], self.hch.dsem[hs], waits=self.hch.free[hs] + self.h_ev))

        issue_h(0)
        issue_h(1)
        for chunk, s in enumerate(O):
            slot, lev = self.load_slab(l, s)
            hs, hl = hl_q.pop(chunk)
            ho = self.hout.take()
            for n, (c0, w) in enumerate(NTS):
                bnk, em = self.mm_fm(slot, lev, kc, c0, w, self.bufA_ev)
                self.wait(dve, [em, hl, self.modAG_ev] + (self.hout.free[ho] if n == 0 else []))
                segs = [(c0, w, 0)] if c0 + w <= TL else [(c0, TL - c0, 0), (TL, T - TL, 1)]
                for (a, ww, t) in segs:
                    ed = self.ev(dve, dve.scalar_tensor_tensor(out=self.hout.t[ho][:, a:a + ww], in0=self.ps[:, bnk, a - c0:a - c0 + ww],
                                                               scalar=self.modG[:, i, chunk:chunk + 1, t], in1=self.hch.t[hs][:, a:a + ww],
                                                               op0=ALU.mult, op1=ALU.add))
                self.ps_release(bnk, ed)
            self.wsl.free[slot] = [em]
            self.hch.free[hs] = [ed]
            if chunk + 2 < len(O):
                issue_h(chunk + 2)
            es_ = self.dma(sp, self.hT[:, chunk, :], self.hout.t[ho][:], self.hout.dsem[ho], waits=[ed])
            self.hout.free[ho] = [es_]
            stores.append(es_)
            if fuse_ss:
                q = self.sqb.take()
                ea = self.op(act, [ed] + self.sqb.free[q], lambda: act.activation(out=self.sqb.t[q][:], in_=self.hout.t[ho][:], func=AF.Square))
                self.hout.free[ho] = [es_, ea]
                if pend is not None:
                    ss_mm(pend, chunk == 1, False)
                pend = (q, ea)
        self.h_ev = stores
        self.bufA_free = [em]
        if fuse_ss:
            self.ss_ev = ss_mm(pend, False, True)
            self.bufA_free = [em, self.ss_ev]
            self.ss_banks = ss_banks
        self.modAG_free = self.modAG_free + [ed]

    def gelu_ps(self, psv, em, A, Bt, M=128):
        act, dve = self.act, self.dve
        e = self.op(act, [em], lambda: act.activation(out=A, in_=psv, func=AF.Square))
        e = self.op(dve, [e], lambda: dve.tensor_scalar(out=A, in0=A, scalar1=0.044715, scalar2=1.0, op0=ALU.mult, op1=ALU.add))
        e = self.op(dve, [e], lambda: dve.tensor_tensor(out=Bt, in0=A, in1=psv, op=ALU.mult))
        e = self.op(act, [e], lambda: act.activation(out=Bt, in_=Bt, func=AF.Sigmoid, scale=1.5957691216057308))
        e = self.op(dve, [e], lambda: dve.tensor_tensor(out=A, in0=Bt, in1=psv, op=ALU.mult))
        return e

    def inproj(self, l):
        act, dve, pe, sp = self.act, self.dve, self.pe, self.sp
        self.norm(l, 1)
        xev = []
        self.qk_ev = []
        self.pvn_ev = []
        zev = self.bufA_ev
        def heads(lst):
          for (slab, is_k) in lst:
              head = slab["idx"]
              slot, lev = self.load_slab(l, slab)
              g = self.gst.take()
              for n, (c0, w) in enumerate(NTS):
                  bnk, em = self.mm_fm(slot, lev, KD, c0, w, zev)
                  hs = self.hch.take()
                  csv = self.hch.t[hs][:, 0:2 * w].rearrange("p (a b) -> p a b", a=2)
                  el = self.dma(sp, csv, self.cs[:, :, c0:c0 + w], self.hch.dsem[hs], waits=self.hch.free[hs])
                  q1 = self.sqb.take()
                  ea = self.op(act, [em] + self.sqb.free[q1], lambda: act.activation(out=self.sqb.t[q1][:, 0:w], in_=self.ps[:, bnk, 0:w], func=AF.Square))
                  b2 = self.ps_take()
                  em2 = self.op(pe, [ea], lambda: pe.matmul(self.ps[:, b2, 0:w], lhsT=self.ones_b[:], rhs=self.sqb.t[q1][:, 0:w], start=True, stop=True))
                  self.sqb.free[q1] = [em2]
                  tq = self.tmpf.take()
                  tm = self.tmpf.t[tq]
                  ea2 = self.op(act, [em2] + self.tmpf.free[tq], lambda: act.activation(out=tm[:, 0:w], in_=self.ps[:, b2, 0:w], func=AF.Sqrt,
                                                                                        scale=1.0 / HD, bias=self.eps_t[:, 0:1]))
                  self.ps_release(b2, ea2)
                  e = self.op(dve, [ea2], lambda: dve.reciprocal(out=tm[:, 0:w], in_=tm[:, 0:w]))
                  e = self.op(dve, [e], lambda: dve.scalar_tensor_tensor(out=tm[:, 352:352 + w], in0=self.ps[:, bnk, 0:w], scalar=self.qkg[:, l, is_k:is_k + 1],
                                                                         in1=tm[:, 0:w], op0=ALU.mult, op1=ALU.mult))
                  self.ps_release(bnk, e)
                  q2 = self.sqb.take()
                  e3 = self.op(dve, [e] + self.sqb.free[q2], lambda: dve.tensor_copy(out=self.sqb.t[q2][:, 0:w], in_=tm[:, 352:352 + w]))
                  b3 = self.ps_take()
                  em3 = self.op(pe, [e3], lambda: pe.matmul(self.ps[:, b3, 0:w], lhsT=self.rot_b[:], rhs=self.sqb.t[q2][:, 0:w], start=True, stop=True))
                  self.sqb.free[q2] = [em3]
                  e4 = self.op(dve, [el, em3, e3], lambda: dve.tensor_tensor(out=tm[:, 704:704 + w], in0=self.ps[:, b3, 0:w], in1=csv[:, 1, :], op=ALU.mult))
                  self.ps_release(b3, e4)
                  e5 = self.op(dve, [e4], lambda: dve.tensor_tensor(out=tm[:, 352:352 + w], in0=tm[:, 352:352 + w], in1=csv[:, 0, :], op=ALU.mult))
                  e6 = self.op(dve, [e5] + (self.gst.free[g] if n == 0 else []),
                               lambda: dve.tensor_tensor(out=self.gst.t[g][:, c0:c0 + w], in0=tm[:, 352:352 + w], in1=tm[:, 704:704 + w], op=ALU.add))
                  self.hch.free[hs] = [e5]
                  self.tmpf.free[tq] = [e6]
              self.wsl.free[slot] = [em]
              gt = self.gst.t[g]
              if is_k:
                  es_ = self.dma(sp, self.Pk[:, head, :], gt[:], self.gst.dsem[g], waits=[e6])
                  base = XK + head * 288
                  self.dma(sp, self.xin[0][:, base:base + 128], gt[:, 0:128], self.gst.dsem[g])
                  self.dma(sp, self.xin[0][:, base + 128:base + 256], gt[:, 896:1024], self.gst.dsem[g])
                  es_ = self.dma(sp, self.xin[0][:, base + 256:base + 288], gt[:, 1024:1056], self.gst.dsem[g])
                  xev.append(es_)
              else:
                  es_ = self.dma(sp, self.Pq[:, head, :], gt[:], self.gst.dsem[g], waits=[e6])
              self.gst.free[g] = [es_]
              self.qk_ev = getattr(self, "qk_ev", []) + [es_]
        heads([(s, 1) for s in self.sl("k")])
        for slab in self.sl("v"):
            hk = slab["idx"]
            slot, lev = self.load_slab(l, slab)
            for tb in range(9):
                bnk, em, M = self.mm_tm(slot, lev, tb, zev)
                q1 = self.sqb.take()
                st = self.sqb.t[q1]
                ea = self.op(act, [em] + self.sqb.free[q1], lambda: act.activation(out=st[0:M, 0:128], in_=self.ps[0:M, bnk, 0:128], func=AF.Identity))
                self.ps_release(bnk, ea)
                es_ = self.dma(sp, self.Pv[128 * tb:128 * tb + M, 128 * hk:128 * hk + 128], st[0:M, 0:128], self.sqb.dsem[q1], waits=[ea])
                if tb == 0:
                    es_ = self.dma(sp, self.xin[0][:, XV + 128 * hk:XV + 128 * hk + 128], st[:, 0:128], self.sqb.dsem[q1])
                if tb == 7:
                    es_ = self.dma(sp, self.xin[0][:, XV + 256 + 128 * hk:XV + 256 + 128 * hk + 128], st[:, 0:128], self.sqb.dsem[q1])
                if tb == 8:
                    es_ = self.dma(sp, self.xin[0][0:TCX, XVC + 128 * hk:XVC + 128 * hk + 128], st[0:TCX, 0:128], self.sqb.dsem[q1])
                self.sqb.free[q1] = [es_]
                xev.append(es_)
            self.wsl.free[slot] = [em]
        heads([(s, 0) for s in self.sl("q")])
        fs = self.sl("f")
        for grp in range(4):
            gsl = []
            fev = []
            for c in range(2):
                slab = fs[2 * grp + c]
                slot, lev = self.load_slab(l, slab)
                g = self.gst.take()
                gsl.append(g)
                for n, (c0, w) in enumerate(NTS):
                    bnk, em = self.mm_fm(slot, lev, KD, c0, w, zev)
                    ea = self.op(act, [em] + (self.gst.free[g] if n == 0 else []),
                                 lambda: act.activation(out=self.gst.t[g][:, c0:c0 + w], in_=self.ps[:, bnk, 0:w], func=AF.Identity))
                    self.ps_release(bnk, ea)
                self.wsl.free[slot] = [em]
                fev.append(ea)
            for tb in range(9):
                M = 128 if tb < 8 else TCX
                bnk = self.ps_take()
                self.wait(pe, fev)
                for c in range(2):
                    mm = pe.matmul(self.ps[0:M, bnk, 0:512], lhsT=self.gst.t[gsl[c]][:, 128 * tb:128 * tb + M], rhs=self.cs256_t[:, c, :],
                                   start=(c == 0), stop=(c == 1))
                em = self.ev(pe, mm)
                q1 = self.sqb.take()
                st = self.sqb.t[q1]
                ed = self.op(dve, [em] + self.sqb.free[q1], lambda: dve.tensor_copy(out=st[0:M, 0:512], in_=self.ps[0:M, bnk, 0:512]))
                self.ps_release(bnk, ed)
                col = (tb % 2) * 2048 + grp * 512
                es_ = self.dma(sp, self.xin[1 + tb // 2][0:M, col:col + 512], st[0:M, 0:512], self.sqb.dsem[q1], waits=[ed])
                self.sqb.free[q1] = [es_]
                xev.append(es_)
            for g in gsl:
                self.gst.free[g] = [em]
        for slab in self.sl("sv"):
            grp = slab["idx"]
            slot, lev = self.load_slab(l, slab)
            hs = self.hch.take()
            lt = self.hch.t[hs]
            self.dma(sp, lt[:, 0:128], self.lng[0:1, l * 1024 + 128 * grp:l * 1024 + 128 * grp + 128].partition_broadcast(128),
                     self.hch.dsem[hs], waits=self.hch.free[hs])
            el = self.dma(sp, lt[:, 128:256], self.lnb[0:1, l * 1024 + 128 * grp:l * 1024 + 128 * grp + 128].partition_broadcast(128),
                          self.hch.dsem[hs])
            for tb in range(9):
                bnk, em, M = self.mm_tm(slot, lev, tb, zev)
                tq = self.tmpf.take()
                tm = self.tmpf.t[tq]
                A, Bt, st = tm[0:M, 0:128], tm[0:M, 128:256], tm[0:M, 256:272]
                psv = self.ps[0:M, bnk, 0:128]
                self.wait(act, self.tmpf.free[tq])
                e = self.gelu_ps(psv, em, A, Bt, M)
                self.ps_release(bnk, e)
                e1 = self.op(dve, [e], lambda: dve.tensor_reduce(out=st[:, 0:1], in_=A, axis=AX.X, op=ALU.add))
                e2 = self.op(act, [e], lambda: act.activation(out=Bt, in_=A, func=AF.Square, accum_out=st[:, 1:2]))
                e = self.op(dve, [e1], lambda: dve.tensor_scalar(out=st[:, 2:3], in0=st[:, 0:1], scalar1=1.0 / 128, scalar2=None, op0=ALU.mult))
                e = self.op(dve, [e], lambda: dve.tensor_tensor(out=st[:, 3:4], in0=st[:, 2:3], in1=st[:, 2:3], op=ALU.mult))
                e = self.op(dve, [e, e2], lambda: dve.scalar_tensor_tensor(out=st[:, 4:5], in0=st[:, 1:2], scalar=1.0 / 128, in1=st[:, 3:4],
                                                                           op0=ALU.mult, op1=ALU.subtract))
                e = self.op(act, [e], lambda: act.activation(out=st[:, 5:6], in_=st[:, 4:5], func=AF.Sqrt, scale=1.0, bias=self.eps_t[0:M, 0:1]))
                e = self.op(dve, [e], lambda: dve.reciprocal(out=st[:, 6:7], in_=st[:, 5:6]))
                e = self.op(dve, [e], lambda: dve.tensor_scalar(out=Bt, in0=A, scalar1=st[:, 2:3], scalar2=st[:, 6:7], op0=ALU.subtract, op1=ALU.mult))
                e = self.op(dve, [e, el], lambda: dve.tensor_tensor(out=Bt, in0=Bt, in1=lt[0:M, 0:128], op=ALU.mult))
                q1 = self.sqb.take()
                so = self.sqb.t[q1]
                e = self.op(dve, [e] + self.sqb.free[q1], lambda: dve.tensor_tensor(out=so[0:M, 0:128], in0=Bt, in1=lt[0:M, 128:256], op=ALU.add))
                self.tmpf.free[tq] = [e]
                es_ = self.dma(sp, self.Pvn[128 * tb:128 * tb + M, 128 * grp:128 * grp + 128], so[0:M, 0:128], self.sqb.dsem[q1], waits=[e])
                if tb == 8:
                    es_ = self.dma(sp, self.xin[0][0:TCX, XVN + 128 * grp:XVN + 128 * grp + 128], so[0:TCX, 0:128], self.sqb.dsem[q1])
                    xev.append(es_)
                self.sqb.free[q1] = [es_]
                self.pvn_ev = getattr(self, "pvn_ev", []) + [es_]
            self.wsl.free[slot] = [em]
            self.hch.free[hs] = [e]
        self.pu_ev = []
        for slab in self.sl("su"):
            grp = slab["idx"]
            slot, lev = self.load_slab(l, slab)
            g = self.gst.take()
            for n, (c0, w) in enumerate(NTS):
                bnk, em = self.mm_fm(slot, lev, KD, c0, w, zev)
                tq = self.tmpf.take()
                tm = self.tmpf.t[tq]
                self.wait(act, self.tmpf.free[tq])
                e = self.gelu_ps(self.ps[:, bnk, 0:w], em, tm[:, 0:w], tm[:, 352:352 + w])
                self.ps_release(bnk, e)
                e = self.op(act, [e] + (self.gst.free[g] if n == 0 else []),
                            lambda: act.activation(out=self.gst.t[g][:, c0:c0 + w], in_=tm[:, 0:w], func=AF.Identity))
                self.tmpf.free[tq] = [e]
            self.wsl.free[slot] = [em]
            es_ = self.dma(sp, self.Pu[:, grp, :], self.gst.t[g][:], self.gst.dsem[g], waits=[e])
            self.gst.free[g] = [es_]
            self.pu_ev.append(es_)
        self.pcx_ev = []
        eedge = []
        for kind, nm in enumerate(("cx", "cc", "cb")):
            pass
        cxs, ccs, cbs = self.sl("cx"), self.sl("cc"), self.sl("cb")
        for i in range(8):
            for kind, slab in enumerate((cxs[i], ccs[i], cbs[i])):
                slot, lev = self.load_slab(l, slab)
                ho = self.hout.take()
                for n, (c0, w) in enumerate(NTS):
                    bnk, em = self.mm_fm(slot, lev, KD, c0, w, zev)
                    ea = self.op(act, [em] + (self.hout.free[ho] if n == 0 else []),
                                 lambda: act.activation(out=self.hout.t[ho][:, c0:c0 + w], in_=self.ps[:, bnk, 0:w], func=AF.Identity))
                    self.ps_release(bnk, ea)
                self.wsl.free[slot] = [em]
                fr = []
                if kind < 2:
                    et = self.xe if kind == 0 else self.ce
                    ht = self.hout.t[ho]
                    self.wait(dve, [ea])
                    dve.tensor_copy(out=et[:, i, 0:1], in_=ht[:, 0:1])
                    dve.tensor_copy(out=et[:, i, 1:2], in_=ht[:, 1023:1024])
                    dve.tensor_copy(out=et[:, i, 2:3], in_=ht[:, 1024:1025])
                    ee = self.ev(dve, dve.tensor_copy(out=et[:, i, 3:4], in_=ht[:, 1055:1056]))
                    fr.append(ee)
                    eedge.append(ee)
                es_ = self.dma(sp, self.Pcx[:, kind, i, :], self.hout.t[ho][:], self.hout.dsem[ho], waits=[ea])
                self.hout.free[ho] = [es_] + fr
                self.pcx_ev.append(es_)
        ez = self.op(dve, eedge, lambda: dve.tensor_tensor(out=self.zeo[:], in0=self.xe[:].rearrange("p a b -> p (a b)"),
                                                           in1=self.ce[:].rearrange("p a b -> p (a b)"), op=ALU.mult))
        ezs = self.mdma(sp, self.xinz[:, :], self.zeo[:], waits=[ez])
        self.exchange(xev, ezs)
        self.pgate_ev = []
        for slab in self.sl("gate"):
            ch, r = slab["idx"], slab["r"]
            slot, lev = self.load_slab(l, slab)
            g = self.gst.take()
            for n, (c0, w) in enumerate(NTS):
                bnk, em = self.mm_fm(slot, lev, KD, c0, w, zev)
                ea = self.op(act, [em] + (self.gst.free[g] if n == 0 else []),
                             lambda: act.activation(out=self.gst.t[g][:, c0:c0 + w], in_=self.ps[:, bnk, 0:w], func=AF.Sigmoid))
                self.ps_release(bnk, ea)
            self.wsl.free[slot] = [em]
            es_ = self.dma(sp, self.Pgate[:, r, ch, :], self.gst.t[g][:], self.gst.dsem[g], waits=[ea])
            self.gst.free[g] = [es_]
            self.pgate_ev.append(es_)
        self.bufA_free = [em]
        self.arena_ok = [em]
        return xev, ezs

    def exchange(self, xev, ezs):
        self.wait(self.pool, xev + [ezs])
        for u in range(XU):
            self.gather(self.xin[u][:, :], self.xall[u], BF16)
        self.gather(self.xinz[:, :], self.zall, F32)
        self.x_ev = [self.gather_flush()]

    def select(self, out, cand_fn, which, waits):
        dve = self.dve
        e = self.op(dve, waits, lambda: dve.tensor_scalar(out=out, in0=cand_fn(0), scalar1=self.oh_t[:, which, 0:1], scalar2=None, op0=ALU.mult))
        for a in range(1, 8):
            e = self.op(dve, [e], lambda: dve.scalar_tensor_tensor(out=out, in0=cand_fn(a), scalar=self.oh_t[:, which, a:a + 1], in1=out,
                                                                   op0=ALU.mult, op1=ALU.add))
        return e

    def branches(self, l):
        act, dve, pe, sp = self.act, self.dve, self.pe, self.sp
        ok = self.arena_ok + self.x_ev
        ys_ev = []
        used = []
        scale = float(HD) ** -0.5
        used = []
        w0 = list(ok)
        zc = self.pg(31, 1, F32)[:, 0:256].rearrange("p (a b) -> p a b", a=8)
        lz = self.mdma(sp, zc, self.zall[:, :, :].rearrange("a p x -> p a x"), waits=w0)
        s1 = self.select(self.selz[:, 0, :], lambda a: zc[:, a, :], 0, [lz])
        s2 = self.select(self.selz[:, 1, :], lambda a: zc[:, a, :], 1, [s1])
        zt = self.pg(3, 3, F32)
        for i in range(8):
            hx, hc_, hb = self.hch.take(), self.hch.take(), self.hch.take()
            lx = self.dma(act, self.hch.t[hx][:], self.Pcx[:, 0, i, :], self.hch.dsem[hx], waits=self.hch.free[hx] + self.pcx_ev)
            lc = self.dma(act, self.hch.t[hc_][:], self.Pcx[:, 1, i, :], self.hch.dsem[hc_], waits=self.hch.free[hc_] + self.pcx_ev)
            lb = self.dma(act, self.hch.t[hb][:], self.Pcx[:, 2, i, :], self.hch.dsem[hb], waits=self.hch.free[hb] + self.pcx_ev)
            X, C_, B_ = self.hch.t[hx], self.hch.t[hc_], self.hch.t[hb]
            ho = self.hout.take()
            yt = self.hout.t[ho]
            e = self.op(dve, [lx, lc, s2] + w0 + used + getattr(self, "zt_free", []), lambda: dve.tensor_tensor(out=zt[:, 1:1025], in0=X[:, 0:TL], in1=C_[:, 0:TL], op=ALU.mult))
            e = self.op(dve, [e], lambda: dve.tensor_tensor(out=zt[:, 1027:1059], in0=X[:, TL:T], in1=C_[:, TL:T], op=ALU.mult))
            dve.tensor_copy(out=zt[:, 0:1], in_=self.selz[:, 0, 4 * i + 1:4 * i + 2])
            dve.tensor_copy(out=zt[:, 1025:1026], in_=self.selz[:, 1, 4 * i + 0:4 * i + 1])
            dve.tensor_copy(out=zt[:, 1026:1027], in_=self.selz[:, 0, 4 * i + 3:4 * i + 4])
            e = self.ev(dve, dve.tensor_copy(out=zt[:, 1059:1060], in_=self.selz[:, 1, 4 * i + 2:4 * i + 3]))
            self.hch.free[hx] = [e]
            self.hch.free[hc_] = [e]
            q1 = self.sqb.take()
            so = self.sqb.t[q1]
            self.wait(dve, [e] + self.hout.free[ho])
            for (o, n_, z0) in ((0, TL, 1), (TL, TCX, 1027)):
                e = self.ev(dve, dve.tensor_scalar(out=yt[:, o:o + n_], in0=zt[:, z0:z0 + n_], scalar1=self.cw_t[:, l, i, 1:2], scalar2=None, op0=ALU.mult))
                self.wait(dve, [e])
                e = self.ev(dve, dve.scalar_tensor_tensor(out=yt[:, o:o + n_], in0=zt[:, z0 - 1:z0 - 1 + n_], scalar=self.cw_t[:, l, i, 0:1], in1=yt[:, o:o + n_],
                                                          op0=ALU.mult, op1=ALU.add))
                self.wait(dve, [e])
                e = self.ev(dve, dve.scalar_tensor_tensor(out=yt[:, o:o + n_], in0=zt[:, z0 + 1:z0 + 1 + n_], scalar=self.cw_t[:, l, i, 2:3], in1=yt[:, o:o + n_],
                                                          op0=ALU.mult, op1=ALU.add))
                self.wait(dve, [e])
            self.zt_free = [e]
            e = self.op(dve, [e, lb] + self.sqb.free[q1], lambda: dve.tensor_tensor(out=so[:], in0=yt[:], in1=B_[:], op=ALU.mult))
            self.hch.free[hb] = [e]
            self.hout.free[ho] = [e]
            es_ = self.dma(act, self.Pys[:, 16 + i, :], so[:], self.sqb.dsem[q1], waits=[e])
            self.sqb.free[q1] = [es_]
            ys_ev.append(es_)
        conv_e = e
        zctx = self.pg(6, 4)[:, 0:4096].rearrange("p (a b) -> p a b", a=2)
        lzc = []
        for r in range(8):
            lzc.append(self.mdma(sp, zctx[32 * (r % 4):32 * (r % 4) + 32, r // 4, :], self.xall[5][pos_of_rank(r), 0:TCX, 0:2048], waits=w0 + used))
        prior = list(ys_ev)
        zp = [self.pg(p0, 2)[:, 0:2048].rearrange("p (b g c x) -> p b g c x", b=2, g=2, c=2) for p0 in (0, 10, 12)]
        zp_sem = [self.sem(f"zp{i}") for i in range(3)] if not hasattr(self, "zp_sem") else self.zp_sem
        self.zp_sem = zp_sem
        zp_free = [[], [], []]
        zp_i = 0
        fo = [self.pg(14), self.pg(15)]
        fo_free = [[], []]
        for gp in range(2):
            banks = [self.ps_take() for _ in range(8)]
            for tp in range(32):
                slot = self.wsl.take()
                ct = self.wsl.t[slot][:].rearrange("p a b -> p (a b)").rearrange("p (b c x) -> p b c x", b=2, c=2)
                lc = self.dma(sp, ct, self.cts[:, 2 * tp:2 * tp + 2, :, :], self.wsl.dsem[slot], waits=self.wsl.free[slot])
                zs = zp_i % 3
                zp_i += 1
                first_d = True
                for bi in range(2):
                    b_ = 2 * tp + bi
                    r, tb = b_ // 8, b_ % 8
                    for gi in range(2):
                        col = (tb % 2) * 2048 + (2 * gp + gi) * 512
                        lzp = self.dma(sp, zp[zs][:, bi, gi, :, :], self.xall[1 + tb // 2][pos_of_rank(r), :, col:col + 512].rearrange("p (c x) -> p c x", c=2),
                                       zp_sem[zs], waits=(zp_free[zs] + w0 + used + prior) if first_d else None)
                        first_d = False
                self.wait(pe, [lc, lzp])
                for bi in range(2):
                    for c in range(2):
                        first = (tp == 0 and bi == 0 and c == 0)
                        last = (tp == 31 and bi == 1 and c == 1)
                        for gi in range(2):
                            for half in range(2):
                                for n2 in range(2):
                                    mm = pe.matmul(self.ps[:, banks[4 * gi + 2 * half + n2], 0:512], lhsT=zp[zs][:, bi, gi, c, 128 * half:128 * half + 128],
                                                   rhs=ct[:, bi, c, 512 * n2:512 * n2 + 512], start=first, stop=last)
                em = self.ev(pe, mm)
                self.wsl.free[slot] = [em]
                zp_free[zs] = [em]
            for gi in range(2):
                grp = 2 * gp + gi
                for half in range(2):
                    fs_ = half
                    self.wait(act, [em] + fo_free[fs_] + w0 + used + prior)
                    for n2 in range(2):
                        ea = self.ev(act, act.activation(out=fo[fs_][:, 512 * n2:512 * n2 + 512], in_=self.ps[:, banks[4 * gi + 2 * half + n2], 0:512], func=AF.Identity))
                        self.ps_release(banks[4 * gi + 2 * half + n2], ea)
                    bnk = self.ps_take()
                    self.wait(pe, lzc)
                    for kb in range(2):
                        for c in range(2):
                            mm = pe.matmul(self.ps[:, bnk, 0:TCX], lhsT=zctx[:, kb, grp * 512 + c * 256 + half * 128:grp * 512 + c * 256 + half * 128 + 128],
                                           rhs=self.ct256_t[:, kb, c, :], start=(kb == 0 and c == 0), stop=(kb == 1 and c == 1))
                    emc = self.ev(pe, mm)
                    ea = self.op(act, [emc], lambda: act.activation(out=fo[fs_][:, TL:T], in_=self.ps[:, bnk, 0:TCX], func=AF.Identity))
                    self.ps_release(bnk, ea)
                    es_ = self.mdma(sp, self.Pys[:, 24 + 2 * grp + half, :], fo[fs_], waits=[ea])
                    fo_free[fs_] = [es_]
                    ys_ev.append(es_)
        f_done = [em, emc, ea] + list(ys_ev)
        ok = ok + f_done + [conv_e]
        used = []
        masks = self.pg(6, 2)[:, 0:2048].rearrange("p (a b) -> p a b", a=4)
        emask = self.mdma(sp, masks, self.masks[:], waits=ok)
        E_pages = [self.pg(8)[:, 0:512], self.pg(8)[:, 512:1024], self.pg(9)[:, 0:512], self.pg(9)[:, 512:1024]]
        E_free = [[] for _ in range(4)]
        E_i = 0
        qt_pages = [self.pg(10)[:, 0:512], self.pg(10)[:, 512:1024], self.pg(11)[:, 0:512]]
        qt_free = [[] for _ in range(3)]
        qt_sem = [self.sem(f"qt{l}_{i}") for i in range(3)] if not hasattr(self, "qt_sem") else self.qt_sem
        self.qt_sem = qt_sem
        qt_i = 0
        den_t = [self.pg(12, 1, F32)[:, 0:512], self.pg(13, 1, F32)[:, 0:512]]
        yst = [self.pg(14)[:, 0:512], self.pg(15)[:, 0:512]]
        yst_free = [[], []]
        yst_sem = [self.sem(f"yst{i}") for i in range(2)] if not hasattr(self, "yst_sem") else self.yst_sem
        self.yst_sem = yst_sem
        y_i = 0
        for hk in range(2):
            kt = self.pg(0, 2)
            vt = self.pg(2, 2)[:, 0:1536].rearrange("p (a b) -> p a b", a=12)
            candK = self.pg(4)[:, 0:1024].rearrange("p (a b) -> p a b", a=8)
            candK2 = self.pg(5)[:, 0:1024].rearrange("p (a b) -> p a b", a=8)
            candV = self.pg(16)[:, 0:1024].rearrange("p (a b) -> p a b", a=8)
            candV2 = self.pg(17)[:, 0:1024].rearrange("p (a b) -> p a b", a=8)
            w0 = ok + used
            lev = [self.mdma(sp, kt[:, 128:1152], self.Pk[:, hk, 0:TL], waits=w0 + self.qk_ev)]
            base = XK + hk * 288
            for r in range(8):
                lev.append(self.mdma(sp, kt[:, 1280 + 32 * r:1280 + 32 * r + 32], self.xall[0][pos_of_rank(r), :, base + 256:base + 288], waits=w0))
            lev.append(self.mdma(sp, candK, self.xall[0][:, :, base + 128:base + 256].rearrange("a p x -> p a x"), waits=w0))
            lev.append(self.mdma(sp, candK2, self.xall[0][:, :, base:base + 128].rearrange("a p x -> p a x"), waits=w0))
            lev.append(self.mdma(sp, vt[:, 1:9, :], self.Pv[0:TL, 128 * hk:128 * hk + 128].rearrange("(b p) c -> p b c", p=128), waits=w0))
            for r in range(8):
                lev.append(self.mdma(sp, vt[32 * (r % 4):32 * (r % 4) + 32, 10 + r // 4, :],
                                     self.xall[0][pos_of_rank(r), 0:TCX, XVC + 128 * hk:XVC + 128 * hk + 128], waits=w0))
            lev.append(self.mdma(sp, candV, self.xall[0][:, :, XV + 256 + 128 * hk:XV + 256 + 128 * hk + 128].rearrange("a p x -> p a x"), waits=w0))
            lev.append(self.mdma(sp, candV2, self.xall[0][:, :, XV + 128 * hk:XV + 128 * hk + 128].rearrange("a p x -> p a x"), waits=w0))
            s1 = self.select(kt[:, 0:128], lambda a: candK[:, a, :], 0, lev)
            s2 = self.select(kt[:, 1152:1280], lambda a: candK2[:, a, :], 1, [s1])
            s3 = self.select(vt[:, 0, :], lambda a: candV[:, a, :], 0, [s2])
            s4 = self.select(vt[:, 9, :], lambda a: candV2[:, a, :], 1, [s3])
            kv_ready = lev + [s4, emask]
            last_pe = None
            for qb in range(9):
                nq = 128 if qb < 8 else TCX
                N = 4 * nq
                c0 = 128 * qb
                qs = qt_i % 3
                qt_i += 1
                qt = qt_pages[qs][:, 0:N].rearrange("p (a b) -> p a b", a=4)
                eq = self.dma(sp, qt, self.Pq[:, 4 * hk:4 * hk + 4, c0:c0 + nq], qt_sem[qs], waits=qt_free[qs] + w0 + self.qk_ev)
                if qb < 8:
                    kbl = [(qb * 128, qb, 2 if qb == 0 else 0), ((qb + 1) * 128, qb + 1, None), ((qb + 2) * 128, qb + 2, 3 if qb == 7 else 1),
                           (1280, 10, None), (1408, 11, None)]
                else:
                    kbl = [(1280, 10, None), (1408, 11, None)]
                bO = self.ps_take()
                bD = self.ps_take()
                Es = []

                def s_mm(j):
                    kc0, vb, mk = kbl[j]
                    bS = self.ps_take()
                    em = self.op(pe, kv_ready + [eq], lambda: pe.matmul(self.ps[:, bS, 0:N], lhsT=kt[:, kc0:kc0 + 128], rhs=qt_pages[qs][:, 0:N], start=True, stop=True))
                    nonlocal E_i
                    es = E_i % 4
                    E_i += 1
                    Et = E_pages[es][:, 0:N]
                    ea = self.op(act, [em] + E_free[es], lambda: act.activation(out=Et, in_=self.ps[:, bS, 0:N], func=AF.Exp, scale=scale))
                    self.ps_release(bS, ea)
                    if mk is not None:
                        ea = self.op(dve, [ea], lambda: dve.tensor_tensor(out=Et, in0=Et, in1=masks[:, mk, 0:N], op=ALU.mult))
                    return es, Et, ea

                nxt = s_mm(0)
                for j in range(len(kbl)):
                    es, Et, ea = nxt
                    if j + 1 < len(kbl):
                        nxt = s_mm(j + 1)
                    vb = kbl[j][1]
                    self.wait(pe, [ea])
                    pe.matmul(self.ps[:, bO, 0:N], lhsT=vt[:, vb, :], rhs=Et, start=(j == 0), stop=(j == len(kbl) - 1))
                    em = self.ev(pe, pe.matmul(self.ps[:, bD, 0:N], lhsT=self.ones_b[:], rhs=Et, start=(j == 0), stop=(j == len(kbl) - 1)))
                    E_free[es] = [em]
                last_pe = em
                qt_free[qs] = [em]
                dt_ = den_t[y_i % 2][:, 0:N]
                self.wait(dve, [em])
                for gq in range(4):
                    e = self.ev(dve, dve.tensor_scalar(out=dt_[:, gq * nq:(gq + 1) * nq], in0=self.ps[:, bD, gq * nq:(gq + 1) * nq],
                                                       scalar1=self.esk[:, l * 8 + 4 * hk + gq:l * 8 + 4 * hk + gq + 1], scalar2=None, op0=ALU.add))
                self.ps_release(bD, e)
                e = self.op(dve, [e], lambda: dve.reciprocal(out=dt_, in_=dt_))
                ys = y_i % 2
                y_i += 1
                yt = yst[ys][:, 0:N]
                e = self.op(dve, [e] + yst_free[ys], lambda: dve.tensor_tensor(out=yt, in0=self.ps[:, bO, 0:N], in1=dt_, op=ALU.mult))
                self.ps_release(bO, e)
                es_ = self.dma(sp, self.Pys[:, 4 * hk:4 * hk + 4, c0:c0 + nq], yt.rearrange("p (a b) -> p a b", a=4), yst_sem[ys], waits=[e])
                yst_free[ys] = [es_]
                ys_ev.append(es_)
            used = [last_pe]
        w0 = ok + used
        wsTb = self.pg(18)[:, 0:1024].rearrange("p (a b) -> p a b", a=8)
        wscb = self.pg(19)[:, 0:512].rearrange("p (c a b) -> p c a b", c=2, a=8)
        bst2 = self.pg(20, 2, F32)[:, 0:1024]
        bsc2 = self.pg(22, 1, F32)[:, 0:256]
        bst = bst2.rearrange("p (a b) -> p a b", a=8)
        bsc = bsc2.rearrange("p (a b) -> p a b", a=8)
        vnctx = self.pg(23, 2)[:, 0:2048].rearrange("p (a b) -> p a b", a=2)
        hs = self.hch.take()
        el = self.dma(sp, self.hch.t[hs][:, 0:1024].rearrange("p (a b) -> p a b", a=8), self.wsT[:, l], self.hch.dsem[hs], waits=self.hch.free[hs])
        e1 = self.op(act, [el] + w0, lambda: act.activation(out=wsTb, in_=self.hch.t[hs][:, 0:1024].rearrange("p (a b) -> p a b", a=8), func=AF.Identity))
        self.hch.free[hs] = [e1]
        hs = self.hch.take()
        el = self.dma(sp, self.hch.t[hs][:, 0:512].rearrange("p (c a b) -> p c a b", c=2, a=8), self.wsc[:, l], self.hch.dsem[hs], waits=self.hch.free[hs])
        e2 = self.op(act, [el] + w0, lambda: act.activation(out=wscb, in_=self.hch.t[hs][:, 0:512].rearrange("p (c a b) -> p c a b", c=2, a=8), func=AF.Identity))
        self.hch.free[hs] = [e2]
        sg = [e1, e2]
        sg.append(self.mdma(sp, bst2, self.bsT[0:1, l * 1024:(l + 1) * 1024].partition_broadcast(128), waits=w0))
        sg.append(self.mdma(sp, bsc2, self.bsc[0:1, l * 256:(l + 1) * 256].partition_broadcast(128), waits=w0))
        for r in range(8):
            sg.append(self.mdma(sp, vnctx[32 * (r % 4):32 * (r % 4) + 32, r // 4, :], self.xall[0][pos_of_rank(r), 0:TCX, XVN:XVN + 1024], waits=w0))
        vnl = [self.pg(25)[:, 0:1024].rearrange("p (a b) -> p a b", a=8), self.pg(26)[:, 0:1024].rearrange("p (a b) -> p a b", a=8)]
        ut = [self.pg(27), self.pg(28)]
        yo = [self.pg(29), self.pg(30)]
        slot_free = [[], []]
        for grp in range(8):
            sl_ = grp % 2
            lv = self.mdma(sp, vnl[sl_], self.Pvn[0:TL, 128 * grp:128 * grp + 128].rearrange("(b p) c -> p b c", p=128), waits=w0 + self.pvn_ev + slot_free[sl_])
            lu = self.mdma(sp, ut[sl_], self.Pu[:, grp, :], waits=w0 + self.pu_ev + slot_free[sl_])
            e = None
            for half in range(2):
                bnk = self.ps_take()
                self.wait(pe, [lv] + sg)
                for t4 in range(4):
                    tb = 4 * half + t4
                    mm = pe.matmul(self.ps[:, bnk, 128 * t4:128 * t4 + 128], lhsT=vnl[sl_][:, tb, :], rhs=wsTb[:, grp, :], start=True, stop=True)
                em = self.ev(pe, mm)
                tq = self.tmpf.take()
                tm = self.tmpf.t[tq]
                self.wait(dve, [em, lu] + sg + self.tmpf.free[tq] + slot_free[sl_])
                for t4 in range(4):
                    tb = 4 * half + t4
                    e = self.ev(dve, dve.tensor_tensor(out=tm[:, 128 * t4:128 * t4 + 128], in0=self.ps[:, bnk, 128 * t4:128 * t4 + 128], in1=bst[:, grp, :], op=ALU.add))
                self.ps_release(bnk, e)
                e = self.op(dve, [e], lambda: dve.tensor_tensor(out=yo[sl_][:, 512 * half:512 * half + 512], in0=tm[:, 0:512],
                                                                in1=ut[sl_][:, 512 * half:512 * half + 512], op=ALU.mult))
                self.tmpf.free[tq] = [e]
            bnk = self.ps_take()
            self.wait(pe, sg)
            for ch in range(2):
                mm = pe.matmul(self.ps[:, bnk, 0:TCX], lhsT=vnctx[:, ch, 128 * grp:128 * grp + 128], rhs=wscb[:, ch, grp, :], start=(ch == 0), stop=(ch == 1))
            em = self.ev(pe, mm)
            tq = self.tmpf.take()
            tm = self.tmpf.t[tq]
            e = self.op(dve, [em] + self.tmpf.free[tq], lambda: dve.tensor_tensor(out=tm[:, 0:TCX], in0=self.ps[:, bnk, 0:TCX], in1=bsc[:, grp, :], op=ALU.add))
            self.ps_release(bnk, e)
            e = self.op(dve, [e], lambda: dve.tensor_tensor(out=yo[sl_][:, TL:T], in0=tm[:, 0:TCX], in1=ut[sl_][:, TL:T], op=ALU.mult))
            self.tmpf.free[tq] = [e]
            es_ = self.mdma(sp, self.Pys[:, 8 + grp, :], yo[sl_], waits=[e])
            slot_free[sl_] = [es_, em]
            ys_ev.append(es_)
        self.ys_ev = ys_ev
        self.arena_done = [em, e]

    def merge(self, l):
        act, dve, sp = self.act, self.dve, self.sp
        evs = []
        for piece in range(4):
            evs.append(self.mdma(sp, self.bufA[:, 8 * piece:8 * (piece + 1), :], self.Pys[:, 8 * piece:8 * (piece + 1), :],
                                 waits=self.ys_ev + self.arena_done))
        self.bufA_ev = evs
        self.pf_limit = len(STREAM) - 1
        br = self.sl("br")
        mstores = []
        gl_q = {}

        def issue_g(idx):
            q1 = self.sqb.take()
            gl_q[idx] = (q1, self.dma(sp, self.sqb.t[q1][:], self.Pgate[:, idx % 4, idx // 4, :], self.sqb.dsem[q1],
                                      waits=self.sqb.free[q1] + self.pgate_ev))

        issue_g(0)
        issue_g(1)
        for ch in range(32):
            ho = self.hout.take()
            acc = self.hout.t[ho]
            for r in range(4):
                slab = br[4 * ch + r]
                slot, lev = self.load_slab(l, slab)
                q1, gl = gl_q.pop(4 * ch + r)
                for n, (c0, w) in enumerate(NTS):
                    bnk, em = self.mm_fm(slot, lev, 8, c0, w, self.bufA_ev, rhs0=8 * r)
                    if r == 0:
                        e = self.op(dve, [em, gl] + (self.hout.free[ho] if n == 0 else []),
                                    lambda: dve.tensor_tensor(out=acc[:, c0:c0 + w], in0=self.ps[:, bnk, 0:w], in1=self.sqb.t[q1][:, c0:c0 + w], op=ALU.mult))
                        self.ps_release(bnk, e)
                    else:
                        tq = self.sil.take()
                        e = self.op(dve, [em, gl] + self.sil.free[tq],
                                    lambda: dve.tensor_tensor(out=self.sil.t[tq][:, 0:w], in0=self.ps[:, bnk, 0:w], in1=self.sqb.t[q1][:, c0:c0 + w], op=ALU.mult))
                        self.ps_release(bnk, e)
                        e = self.op(dve, [e], lambda: dve.tensor_tensor(out=acc[:, c0:c0 + w], in0=acc[:, c0:c0 + w], in1=self.sil.t[tq][:, 0:w], op=ALU.add))
                        self.sil.free[tq] = [e]
                self.wsl.free[slot] = [em]
                self.sqb.free[q1] = [e]
                if 4 * ch + r + 2 < 128:
                    issue_g(4 * ch + r + 2)
            g = self.gst.take()
            ea = self.op(act, [e] + self.gst.free[g], lambda: act.activation(out=self.gst.t[g][:], in_=acc[:], func=AF.Identity))
            self.hout.free[ho] = [ea]
            es_ = self.dma(sp, self.mT[:, ch, :], self.gst.t[g][:], self.gst.dsem[g], waits=[ea])
            self.gst.free[g] = [es_]
            mstores.append(es_)
        evs = []
        for piece in range(4):
            evs.append(self.mdma(sp, self.bufA[:, 8 * piece:8 * (piece + 1), :], self.mT[:, 8 * piece:8 * (piece + 1), :], waits=mstores + [em]))
        self.bufA_ev = evs
        self.down_proj(l, self.sl("wo"), KD, 1)

    def layer(self, l):
        L = self.nlayers
        self.pf_limit = LAST_GATE
        self.layer_mods(l)
        self.ffn(l, 0)
        if self.stop == "ffn1":
            return False
        self.inproj(l)
        if self.nrounds is None:
            self.pool_rounds(l * NR_L + R1, (l + 1) * NR_L)
            if l + 1 < L:
                self.pool_rounds((l + 1) * NR_L, (l + 1) * NR_L + R1)
        if self.stop == "inproj":
            return False
        self.branches(l)
        if self.stop == "branches":
            return False
        self.merge(l)
        if self.stop == "mixer":
            return False
        self.ffn(l, 1)
        return True

    def finish(self):
        sp = self.sp
        evs = []
        for piece in range(4):
            evs.append(self.mdma(sp, self.outT[:, 8 * piece:8 * (piece + 1), :], self.hT[:, 8 * piece:8 * (piece + 1), 0:TL],
                                 waits=self.h_ev))
        for name in self.dbg:
            src = getattr(self, name)
            dst = self.dram("dbg_" + name, list(src.shape), src.dtype, "ExternalOutput")
            w = self.h_ev + getattr(self, "ys_ev", []) + getattr(self, "x_ev", []) + getattr(self, "pgate_ev", []) + getattr(self, "pcx_ev", []) \
                + getattr(self, "pu_ev", []) + getattr(self, "pvn_ev", []) + getattr(self, "qk_ev", [])
            evs.append(self.mdma(sp, dst.ap().opt(), src.ap().opt(), waits=w))
        self.wait(sp, evs)
        if self.pool_cc_pending is not None:
            self.wait(self.pool, self.pool_cc_pending)


def kernel(**inputs):
    nl = DEPTH
    maps = host_inputs(inputs, nl)
    nc = Builder(nl, "full").build()
    res = run_bass_kernel_spmd(nc, maps, core_ids=list(range(NCORES)))
    out = np.empty((1, SEQ, D), np.float32)
    for c in range(NCORES):
        oT = np.asarray(res.results[c]["outT"])
        out[0, TL * c:TL * (c + 1), :] = oT.transpose(2, 1, 0).reshape(TL, D)
    return out
```
